# Optimizing a Trainium2 kernel written in Bass

```python
import jax, jax.numpy as jnp
from jax import lax
import numpy as np

D_MODEL = 1024
BATCH = 8
SEQ = 4096
DEPTH = 1

PLE_DIM = 256
RET_HEADS = 4
RET_HEAD_DIM = 128
RET_WIDTH = RET_HEADS * RET_HEAD_DIM
FOX_HEADS = 8
FOX_HEAD_DIM = 64
FOX_WIDTH = FOX_HEADS * FOX_HEAD_DIM
D_FF = 2816
RET_CHUNK = 128
Q_BLOCK = 128
ROPE_BASE = 10000.0
EPS = 1e-6
IN_SIZES = (RET_WIDTH,) * 4 + (FOX_WIDTH,) * 3 + (FOX_HEADS,)
IN_COLS = 4 * RET_WIDTH + 3 * FOX_WIDTH + FOX_HEADS

kernel_name = 'hybrid_retention_forgetting_macaron_block'


def rms_norm(x, g):
    x32 = x.astype(jnp.float32)
    y = x32 * lax.rsqrt(jnp.mean(x32 * x32, axis=-1, keepdims=True) + EPS)
    return (y * g.astype(jnp.float32)).astype(x.dtype)


def swiglu(x, w_gate, w_up, w_down):
    return (jax.nn.silu(x @ w_gate) * (x @ w_up)) @ w_down


def rotary(x, pos):
    half = x.shape[-1] // 2
    inv_freq = 1.0 / (ROPE_BASE ** (jnp.arange(half, dtype=jnp.float32) / half))
    ang = pos.astype(jnp.float32)[..., None] * inv_freq
    cos = jnp.cos(ang)[:, :, None, :]
    sin = jnp.sin(ang)[:, :, None, :]
    x1 = x[..., 0::2]
    x2 = x[..., 1::2]
    return jnp.stack([x1 * cos - x2 * sin, x1 * sin + x2 * cos], axis=-1).reshape(x.shape)


def chunkwise_retention(q, k, v):
    b, s, h, dk = q.shape
    dv = v.shape[-1]
    c = RET_CHUNK
    n = s // c
    to_chunks = lambda t: t.reshape(b, n, c, h, t.shape[-1]).transpose(0, 3, 1, 2, 4)
    q = to_chunks(q)
    k = to_chunks(k) * (dk ** -0.5)
    v = to_chunks(v)
    log_gamma = jnp.log1p(-jnp.exp2(-5.0 - jnp.arange(h, dtype=jnp.float32)))
    idx = jnp.arange(c, dtype=jnp.float32)
    diff = idx[:, None] - idx[None, :]
    dmask = jnp.where(diff >= 0, jnp.exp(log_gamma[:, None, None] * jnp.maximum(diff, 0.0)), 0.0)
    scores = jnp.einsum('bhncd,bhnmd->bhncm', q, k) * dmask[None, :, None]
    y_inner = jnp.einsum('bhncm,bhnme->bhnce', scores, v)
    k_dec = k * jnp.exp(log_gamma[:, None] * (c - 1 - idx))[None, :, None, :, None]
    kv = jnp.einsum('bhnmd,bhnme->nbhde', k_dec, v)
    chunk_decay = jnp.exp(log_gamma * c)[None, :, None, None]

    def step(state, kv_n):
        return chunk_decay * state + kv_n, state

    _, states = lax.scan(step, jnp.zeros((b, h, dk, dv), jnp.float32), kv)
    q_dec = q * jnp.exp(log_gamma[:, None] * (idx + 1.0))[None, :, None, :, None]
    y_cross = jnp.einsum('bhncd,nbhde->bhnce', q_dec, states)
    y = y_inner + y_cross
    return y.transpose(0, 2, 3, 1, 4).reshape(b, s, h, dv)


def head_group_norm(y):
    mu = jnp.mean(y, axis=-1, keepdims=True)
    var = jnp.mean(jnp.square(y - mu), axis=-1, keepdims=True)
    return (y - mu) * lax.rsqrt(var + EPS)


def forgetting_attention(q, k, v, log_f):
    b, s, h, d = q.shape
    nb = s // Q_BLOCK
    cum = jnp.cumsum(log_f, axis=1).transpose(0, 2, 1)
    qh = q.transpose(0, 2, 1, 3) * (d ** -0.5)
    kh = k.transpose(0, 2, 1, 3)
    vh = v.transpose(0, 2, 1, 3)
    q_blocks = qh.reshape(b, h, nb, Q_BLOCK, d).transpose(2, 0, 1, 3, 4)
    c_blocks = cum.reshape(b, h, nb, Q_BLOCK).transpose(2, 0, 1, 3)
    k_pos = jnp.arange(s)

    def block(args):
        q_n, c_n, n = args
        logits = jnp.einsum('bhqd,bhkd->bhqk', q_n, kh) + c_n[..., None] - cum[:, :, None, :]
        q_pos = n * Q_BLOCK + jnp.arange(Q_BLOCK)
        logits = jnp.where(k_pos[None, :] <= q_pos[:, None], logits, -jnp.inf)
        probs = jax.nn.softmax(logits, axis=-1)
        return jnp.einsum('bhqk,bhkd->bhqd', probs, vh)

    o = lax.map(block, (q_blocks, c_blocks, jnp.arange(nb)))
    return o.transpose(1, 0, 3, 2, 4).reshape(b, s, h, d)


def setup_inputs(seed: int = 0) -> dict:
    key = jax.random.key(seed)
    ks = jax.random.split(key, 24)
    f32 = jnp.float32

    def nrm(k, shape, fan_in):
        return jax.random.normal(k, shape, f32) * (fan_in ** -0.5)

    def gain(k, shape):
        return 1.0 + 0.05 * jax.random.normal(k, shape, f32)

    x = jax.random.normal(ks[0], (BATCH, SEQ, D_MODEL), f32)
    p = jax.random.normal(ks[1], (DEPTH, BATCH, SEQ, PLE_DIM), f32)
    positions = jnp.broadcast_to(jnp.arange(SEQ, dtype=jnp.int32)[None, :], (BATCH, SEQ))
    return {
        'x': x,
        'p': p,
        'positions': positions,
        'ln_ffn1': gain(ks[2], (DEPTH, D_MODEL)),
        'w_ffn1_gate': nrm(ks[3], (DEPTH, D_MODEL, D_FF), D_MODEL),
        'w_ffn1_up': nrm(ks[4], (DEPTH, D_MODEL, D_FF), D_MODEL),
        'w_ffn1_down': nrm(ks[5], (DEPTH, D_FF, D_MODEL), D_FF),
        'ln_mix': gain(ks[6], (DEPTH, D_MODEL)),
        'w_in': nrm(ks[7], (DEPTH, D_MODEL, IN_COLS), D_MODEL),
        'b_forget': jax.random.uniform(ks[8], (DEPTH, FOX_HEADS), f32, 1.0, 6.0),
        'w_merge': nrm(ks[9], (DEPTH, D_MODEL, 2 * D_MODEL), D_MODEL),
        'b_merge': 0.02 * jax.random.normal(ks[10], (DEPTH, 2 * D_MODEL), f32),
        'w_ret_out': nrm(ks[11], (DEPTH, RET_WIDTH, D_MODEL), RET_WIDTH),
        'w_fox_out': nrm(ks[12], (DEPTH, FOX_WIDTH, D_MODEL), FOX_WIDTH),
        'w_out': nrm(ks[13], (DEPTH, D_MODEL, D_MODEL), D_MODEL),
        'ln_ffn2': gain(ks[14], (DEPTH, D_MODEL)),
        'w_ffn2_gate': nrm(ks[15], (DEPTH, D_MODEL, D_FF), D_MODEL),
        'w_ffn2_up': nrm(ks[16], (DEPTH, D_MODEL, D_FF), D_MODEL),
        'w_ffn2_down': nrm(ks[17], (DEPTH, D_FF, D_MODEL), D_FF),
        'ln_ple': gain(ks[18], (DEPTH, D_MODEL)),
        'w_ple': nrm(ks[19], (DEPTH, PLE_DIM, D_MODEL), PLE_DIM),
        'w_ple_gate': nrm(ks[20], (DEPTH, D_MODEL, D_MODEL), D_MODEL),
        'ln_final': gain(ks[21], (D_MODEL,)),
    }


def reference(x, p, positions, ln_ffn1, w_ffn1_gate, w_ffn1_up, w_ffn1_down, ln_mix, w_in,
              b_forget, w_merge, b_merge, w_ret_out, w_fox_out, w_out, ln_ffn2, w_ffn2_gate,
              w_ffn2_up, w_ffn2_down, ln_ple, w_ple, w_ple_gate, ln_final):
    dt = x.dtype
    b, s, _ = x.shape
    split_at = np.cumsum(IN_SIZES)[:-1].tolist()
    h = x
    for i in range(DEPTH):
        h = h + 0.5 * swiglu(rms_norm(h, ln_ffn1[i]), w_ffn1_gate[i], w_ffn1_up[i], w_ffn1_down[i])

        u = rms_norm(h, ln_mix[i])
        proj = u @ w_in[i]
        r_q, r_k, r_v, r_g, f_q, f_k, f_v, f_f = jnp.split(proj, split_at, axis=-1)

        rq = rotary(r_q.astype(jnp.float32).reshape(b, s, RET_HEADS, RET_HEAD_DIM), positions)
        rk = rotary(r_k.astype(jnp.float32).reshape(b, s, RET_HEADS, RET_HEAD_DIM), positions)
        rv = r_v.astype(jnp.float32).reshape(b, s, RET_HEADS, RET_HEAD_DIM)
        y_ret = head_group_norm(chunkwise_retention(rq, rk, rv)).reshape(b, s, RET_WIDTH)
        y_ret = (y_ret * jax.nn.silu(r_g.astype(jnp.float32))).astype(dt)
        z_a = y_ret @ w_ret_out[i]

        log_f = jax.nn.log_sigmoid(f_f.astype(jnp.float32) + b_forget[i].astype(jnp.float32))
        fq = f_q.astype(jnp.float32).reshape(b, s, FOX_HEADS, FOX_HEAD_DIM)
        fk = f_k.astype(jnp.float32).reshape(b, s, FOX_HEADS, FOX_HEAD_DIM)
        fv = f_v.astype(jnp.float32).reshape(b, s, FOX_HEADS, FOX_HEAD_DIM)
        y_fox = forgetting_attention(fq, fk, fv, log_f).reshape(b, s, FOX_WIDTH).astype(dt)
        z_b = y_fox @ w_fox_out[i]

        gates = jax.nn.sigmoid(u @ w_merge[i] + b_merge[i])
        g_a, g_b = jnp.split(gates, 2, axis=-1)
        h = h + (g_a * z_a + g_b * z_b) @ w_out[i]

        h = h + 0.5 * swiglu(rms_norm(h, ln_ffn2[i]), w_ffn2_gate[i], w_ffn2_up[i], w_ffn2_down[i])

        ple_gate = jax.nn.sigmoid(rms_norm(h, ln_ple[i]) @ w_ple_gate[i])
        h = h + ple_gate * (p[i].astype(dt) @ w_ple[i])
    return rms_norm(h, ln_final)
```

```python
import numpy as np
from contextlib import ExitStack
import concourse.bass as bass
import concourse.mybir as mybir
from concourse.bass_utils import run_bass_kernel_spmd

F32 = mybir.dt.float32
BF16 = mybir.dt.bfloat16
I32 = mybir.dt.int32
AF = mybir.ActivationFunctionType
ALU = mybir.AluOpType
AX = mybir.AxisListType

D = 1024
DFF = 2816
KC = D // 128
FC = DFF // 128
PLE = 256
RH, RD = 4, 128
FH, FD = 8, 64
T = 256
NB = T // 128
UNIT_BLOCKS = 16
UNIT_COLS = UNIT_BLOCKS * 128
NSLOT = 4
EPS = 1e-6
IN_COLS = 4 * 512 + 3 * 512 + 8
TWO_PI_HI = 6.28125
TWO_PI_LO = float(2.0 * np.pi - 6.28125)
MASK_NEG = -30000.0


class Res:
    __slots__ = ("name", "last_write", "reads", "dsem", "dcount", "excl")

    def __init__(self, name):
        self.name = name
        self.excl = False
        self.last_write = None
        self.reads = []
        self.dsem = {}
        self.dcount = {}


class Op:
    __slots__ = ("eng", "fn", "deps", "epoch", "signal", "count", "is_dma", "dres", "dval", "idx")


COMPUTE = ("pe", "act", "dve", "pool")


class Prog:
    def __init__(self):
        self.ops = {e: [] for e in ("pe", "act", "dve", "pool", "sp")}
        self.epoch = 0
        self.n_ops = 0

    def op(self, eng, fn, reads=(), writes=(), extra=(), dma=None):
        o = Op()
        o.eng = eng
        o.fn = fn
        o.epoch = self.epoch
        o.signal = False
        o.count = 0
        o.is_dma = dma is not None
        o.dres = dma
        o.dval = 0
        o.idx = self.n_ops
        self.n_ops += 1
        deps = []
        xr = [r for r in reads if r.excl]
        if xr:
            reads = [r for r in reads if not r.excl]
            writes = list(writes) + [r for r in xr if r not in writes]
        for r in reads:
            if r.last_write is not None:
                deps.append(r.last_write)
        for w in writes:
            if w.last_write is not None:
                deps.append(w.last_write)
            deps.extend(w.reads)
        deps.extend(extra)
        seen = set()
        od = []
        for d in deps:
            if d is None or d is o or id(d) in seen:
                continue
            seen.add(id(d))
            if (not d.is_dma) and d.eng == "pe" and eng == "pe" and not o.is_dma:
                continue
            od.append(d)
        o.deps = od
        if o.is_dma:
            dma.dcount[eng] = dma.dcount.get(eng, 0) + 1
            o.dval = dma.dcount[eng] * 16
        for r in reads:
            if not o.is_dma:
                r.reads = [x for x in r.reads if x.is_dma or x.eng != eng]
            r.reads.append(o)
        for w in writes:
            w.last_write = o
            w.reads = []
        self.ops[eng].append(o)
        return o


def build_program(S):
    NT = S // T
    NBLK = S // 128
    nc = bass.Bass("TRN2", target_bir_lowering=False)
    P = Prog()
    plan = []
    st = ExitStack()

    def dram_in(name, shape, dt):
        return nc.dram_tensor(name, list(shape), dt, kind="ExternalInput")

    x_d = dram_in("x", [S, D], F32)
    p_d = dram_in("p", [S, PLE], F32)
    pos_d = dram_in("pos", [128, NBLK], I32)
    cols_d = dram_in("cols", [128, 64], F32)
    lnf_d = dram_in("lnf", [128, D], F32)
    cf_d = dram_in("cf32", [128, 2048], F32)
    cb_d = dram_in("cbf", [128, 1536], F32)
    out_d = nc.dram_tensor("out", [S, D], F32, kind="ExternalOutput")

    def sb(name, shape, dt):
        return st.enter_context(nc.sbuf_tensor("s_" + name, list(shape), dt))

    def ps(name, shape, dt):
        return st.enter_context(nc.psum_tensor(name, list(shape), dt))

    kT = sb("kT", [128, 4, S], BF16)
    vaug = sb("vaug", [128, NBLK, FH, 66], BF16)
    ncol = sb("ncol", [128, NBLK, 8], F32)
    Sst = sb("Sst", [128, 512], F32)
    Sbf = sb("Sbf", [128, 512], BF16)
    cols = sb("cols", [128, 64], F32)
    lnf = sb("lnf", [128, D], F32)
    cf = sb("cf", [128, 2048], F32)
    cb = sb("cb", [128, 1536], BF16)
    posi = sb("posi", [128, NBLK], I32)
    posf = sb("posf", [128, NBLK], F32)
    carry = sb("carry", [8, 2], F32)
    ones8 = sb("ones8", [8, T], F32)
    c3s = sb("c3s", [96, T], BF16)
    c3t = sb("c3t", [8, 3, T], BF16)
    wring = sb("wring", [128, NSLOT, UNIT_COLS], BF16)
    h = sb("h", [128, KC, T], F32)
    xn = sb("xn", [128, KC, T], BF16)
    a = sb("a", [128, FC, T], BF16)
    xs = sb("xs", [128, 2, D], F32)
    sq = sb("sq", [128, 2, T], BF16)
    sd = sb("sd", [128, T], F32)
    rstd = sb("rstd", [128, T], F32)
    sgt = sb("sgt", [128, 3, T], F32)
    t1 = sb("t1", [128, 2, T], F32)
    t2 = sb("t2", [128, 2, T], F32)
    qr = sb("qr", [128, NB, 512], BF16)
    kr = sb("kr", [128, NB, 512], BF16)
    kdec = sb("kdec", [128, NB, 512], BF16)
    vt = sb("vt", [128, NB, 512], BF16)
    sgate = sb("sgate", [128, NB, 512], F32)
    qT = sb("qT", [128, NB, 512], BF16)
    qdT = sb("qdT", [128, NB, 512], BF16)
    kTr = sb("kTr", [128, NB, 512], BF16)
    AT = sb("AT", [128, 512], BF16)
    ycp = sb("ycp", [128, 512], F32)
    yn = sb("yn", [128, 512], F32)
    yg = sb("yg", [128, 512], BF16)
    st4 = sb("st4", [128, 32], F32)
    yretT = sb("yretT", [128, 4, T], BF16)
    yfoxT = sb("yfoxT", [128, 4, T], BF16)
    qpad = sb("qpad", [128, FH, T], BF16)
    fsb = sb("fsb", [8, 4, T], F32)
    pt = sb("pt", [128, 4, T], BF16)
    osb = sb("osb", [64, 2, T], F32)
    rl = sb("rl", [1, 2, T], F32)
    rl2 = sb("rl2", [128, 2, T], BF16)
    mg = sb("mg", [128, KC, T], BF16)
    pst = sb("pst", [128, 1, PLE], F32)
    pT = sb("pT", [128, 2, T], BF16)
    rot = sb("rot", [128, 4, 512], F32)
    roti = sb("roti", [128, 512], I32)
    ho = sb("ho", [128, 1, D], F32)
    fst = sb("fst", [128, 8], F32)
    banks = [ps(f"bk{i}", [128, 512], F32) for i in range(7)]
    bkb = ps("bkb", [128, 1024], BF16)

    R = {}

    def res(name):
        if name not in R:
            R[name] = Res(name)
        return R[name]

    bank_r = [res(f"bank{i}") for i in range(7)]
    bkb_r = [res("bkb0"), res("bkb0")]
    for r_ in bank_r + bkb_r:
        r_.excl = True
    slot_r = [res(f"slot{i}") for i in range(NSLOT)]
    h_r = [res(f"h{c}") for c in range(KC)]
    a_r = [res(f"a{c}") for c in range(FC)]
    const_r = res("const")
    xn_r = [res(f"xn{c}") for c in range(KC)]

    C_G1, C_GMIX, C_G2, C_GPLE = 0, 8, 16, 24
    C_BM = 32
    C_NEGB = 48
    CF_IDENT = 0
    CF_MASKT = 128
    CF_QDEC = 640
    CF_KDEC = 1152
    CF_INVF = 1664
    CF_ONES = 1920
    CB_IDENT = 0
    CB_ONES = 128
    CB_TRI = 256
    CB_SEL = 384
    CB_RSEL = 1408

    class WS:
        def __init__(self):
            self.blk = 0
            self.tile = 0
            self.loaded = 0
            self.nu = None
            self.wb_ops = {}
            self.recording = True

        def units_per_tile(self):
            return self.nu

        def ensure(self, gunit):
            lim = gunit + NSLOT - 1
            total = None if self.nu is None else self.nu * NT
            while self.loaded <= lim and (total is None or self.loaded < total):
                self.emit_load(self.loaded)
                self.loaded += 1

        def emit_load(self, g):
            slot = g % NSLOT
            sr = slot_r[slot]
            if self.nu is None or g < self.nu:
                u = g
                src = wst_d[u]
                P.op("pool", lambda e, s=slot, sr_=src: e.dma_start(out=wring[:, s, :], in_=sr_),
                     writes=[sr], dma=sr)
                o = P.op("sp", lambda e, s=slot, u_=u: e.dma_start(out=wscr_d[u_], in_=wring[:, s, :]),
                         reads=[sr], dma=sr)
                self.wb_ops[u] = o
            else:
                u = g % self.nu
                P.op("sp", lambda e, s=slot, u_=u: e.dma_start(out=wring[:, s, :], in_=wscr_d[u_]),
                     writes=[sr], extra=[self.wb_ops[u]], dma=sr)

        def take(self, n, pieces):
            if self.blk % n:
                self.blk += n - self.blk % n
            if self.recording:
                plan.append((self.blk, n, pieces))
            base = 0 if self.nu is None else self.tile * self.nu
            u = self.blk // UNIT_BLOCKS
            g = base + u
            self.ensure(g)
            off = (self.blk % UNIT_BLOCKS) * 128
            slot = g % NSLOT
            self.blk += n
            return wring[:, slot, off:off + n * 128], slot_r[slot]

        def end_tile(self):
            nu = (self.blk + UNIT_BLOCKS - 1) // UNIT_BLOCKS
            if self.nu is None:
                self.nu = nu
            assert nu == self.nu
            self.recording = False
            self.blk = 0
            self.tile += 1

    ws = WS()
    NU = count_units()
    wst_d = dram_in("wst", [NU, 128, UNIT_COLS], F32)
    wscr_d = nc.dram_tensor("wscr", [NU, 128, UNIT_COLS], BF16, kind="Internal")
    ws.nu = NU

    bank_rot = [0]

    def next_bank(lo=0, hi=6):
        b = bank_rot[0]
        if b < lo or b >= hi:
            b = lo
        bank_rot[0] = b + 1
        return b

    def mm(out, lhsT, rhs, start, stop, reads, writes):
        P.op("pe", lambda e: e.matmul(out, lhsT=lhsT, rhs=rhs, start=start, stop=stop),
             reads=reads, writes=writes)

    def W(name, r0, nr, c0, ncw, dr=0, dc=0):
        return (name, r0, nr, c0, ncw, dr, dc)

    def col(c):
        return cols[:, c:c + 1]

    def init():
        for (dst, src) in ((cols, cols_d), (lnf, lnf_d), (cf, cf_d), (posi, pos_d)):
            P.op("sp", lambda e, d_=dst, s_=src: e.dma_start(out=d_[:], in_=s_.ap()),
                 writes=[const_r], dma=const_r)
        P.op("pool", lambda e: e.dma_start(out=cb[:], in_=cb_d.ap()), writes=[res("cb")], dma=res("cb"))
        P.op("dve", lambda e: e.tensor_copy(out=posf[:], in_=posi[:]), reads=[const_r], writes=[res("posf")])
        P.op("dve", lambda e: e.memset(Sst[:], 0.0), writes=[res("Sst")])
        P.op("dve", lambda e: e.memset(Sbf[:], 0.0), writes=[res("Sbf")])
        P.op("dve", lambda e: e.memset(carry[:], 0.0), writes=[res("carry")])
        P.op("dve", lambda e: e.memset(ones8[:], 1.0), writes=[res("ones8")])
        P.op("dve", lambda e: e.memset(c3s[:], 0.0), writes=[res("c3s")])
        P.op("dve", lambda e: e.memset(rl2[:], 0.0), writes=[res("rl20"), res("rl21")])
        P.op("dve", lambda e: e.memset(qpad[:], 0.0), writes=[res(f"qpad{i}") for i in range(FH)])
        P.op("pool", lambda e: e.memset(vaug[:], 1.0), writes=[res("vaug")])

    done_loads = set()

    def emit_x_load(ti, j):
        if ("x", ti, j) in done_loads:
            return
        done_loads.add(("x", ti, j))
        slot = j % 2
        t0 = ti * T + j * 128
        P.op("pool", lambda e: e.dma_start(out=xs[:, slot, :], in_=x_d[t0:t0 + 128, :]),
             writes=[res(f"xs{slot}")], dma=res(f"xs{slot}"))

    def emit_p_load(ti, j):
        if ("p", ti, j) in done_loads:
            return
        done_loads.add(("p", ti, j))
        slot = 0
        t0 = ti * T + j * 128
        P.op("pool", lambda e: e.dma_start(out=pst[:, slot, :], in_=p_d[t0:t0 + 128, :]),
             writes=[res(f"pst{slot}")], dma=res(f"pst{slot}"))

    def x_transpose(ti):
        for j in range(NB):
            slot = j % 2
            emit_x_load(ti, j)
            for half in range(2):
                b = next_bank()
                for cc in range(4):
                    c = half * 4 + cc
                    P.op("pe", lambda e, b=b, cc=cc, c=c, slot=slot: e.transpose(
                        out=banks[b][:, cc * 128:(cc + 1) * 128], in_=xs[:, slot, c * 128:(c + 1) * 128],
                        identity=cf[:, CF_IDENT:CF_IDENT + 128]),
                        reads=[res(f"xs{slot}"), const_r], writes=[bank_r[b]])
                P.op("act", lambda e, b=b, half=half, j=j: e.activation(
                    out=h[:, half * 4:half * 4 + 4, j * 128:(j + 1) * 128],
                    in_=banks[b][:, :].rearrange("p (c t) -> p c t", c=4), func=AF.Copy),
                    reads=[bank_r[b]], writes=[h_r[half * 4 + i] for i in range(4)])
                if j == NB - 1:
                    for cc in range(4):
                        ss_accum(half * 4 + cc)

    def ss_accum(c):
        s_ = c % 2
        if c % 2 == 0:
            P.op("act", lambda e: e.activation(out=sq[:, s_, :], in_=h[:, c, :], func=AF.Square),
                 reads=[h_r[c]], writes=[res(f"sq{s_}")])
        else:
            P.op("dve", lambda e: e.tensor_tensor(out=sq[:, s_, :], in0=h[:, c, :], in1=h[:, c, :], op=ALU.mult),
                 reads=[h_r[c]], writes=[res(f"sq{s_}")])
        mm(banks[6][:, 0:T], cb[:, CB_ONES:CB_ONES + 128], sq[:, s_, :], c == 0, c == KC - 1,
           [res(f"sq{s_}"), res("cb")], [bank_r[6]])

    def rmsnorm(gcol0):
        b = 6
        P.op("act", lambda e: e.activation(out=sd[:], in_=banks[b][:, 0:T], func=AF.Sqrt,
                                           bias=cols[:, 63:64], scale=1.0 / D),
             reads=[bank_r[b], const_r], writes=[res("sd")])
        P.op("dve", lambda e: e.reciprocal(out=rstd[:], in_=sd[:]), reads=[res("sd")], writes=[res("rstd")])
        for c in range(KC):
            P.op("dve", lambda e, c=c: e.scalar_tensor_tensor(
                out=xn[:, c, :], in0=h[:, c, :], scalar=col(gcol0 + c), in1=rstd[:],
                op0=ALU.mult, op1=ALU.mult),
                reads=[h_r[c], res("rstd"), const_r], writes=[xn_r[c]])

    def ffn(wg, wu, wd):
        for m in range(FC):
            bg = next_bank()
            bu = next_bank()
            for k in range(KC):
                wap, wr = ws.take(1, [W(wg, k * 128, 128, m * 128, 128)])
                mm(banks[bg][:, 0:T], wap, xn[:, k, :], k == 0, k == KC - 1, [wr, xn_r[k]], [bank_r[bg]])
            for k in range(KC):
                wap, wr = ws.take(1, [W(wu, k * 128, 128, m * 128, 128)])
                mm(banks[bu][:, 0:T], wap, xn[:, k, :], k == 0, k == KC - 1, [wr, xn_r[k]], [bank_r[bu]])
            s = m % 3
            P.op("act", lambda e, bg=bg, s=s: e.activation(out=sgt[:, s, :], in_=banks[bg][:, 0:T], func=AF.Silu),
                 reads=[bank_r[bg]], writes=[res(f"sgt{s}")])
            P.op("dve", lambda e, bu=bu, s=s, m=m: e.tensor_tensor(out=a[:, m, :], in0=banks[bu][:, 0:T],
                                                                  in1=sgt[:, s, :], op=ALU.mult),
                 reads=[bank_r[bu], res(f"sgt{s}")], writes=[a_r[m]])
        for m in range(KC):
            b = next_bank()
            for k in range(FC):
                wap, wr = ws.take(1, [W(wd, k * 128, 128, m * 128, 128)])
                mm(banks[b][:, 0:T], wap, a[:, k, :], k == 0, k == FC - 1, [wr, a_r[k]], [bank_r[b]])
            P.op("dve", lambda e, b=b, m=m: e.scalar_tensor_tensor(
                out=h[:, m, :], in0=banks[b][:, 0:T], scalar=0.5, in1=h[:, m, :], op0=ALU.mult, op1=ALU.add),
                reads=[bank_r[b], h_r[m]], writes=[h_r[m]])
            if m > 0:
                ss_accum(m - 1)
        ss_accum(KC - 1)

    def rotary_tables(ti):
        rr = res("rotw")
        t1f = t1[:].rearrange("p a b -> p (a b)")
        t2f = t2[:].rearrange("p a b -> p (a b)")
        t1rs = [res("t1a"), res("t1b")]
        t2rs = [res("t2a"), res("t2b")]
        for j in range(NB):
            blk = ti * NB + j
            P.op("dve", lambda e, j=j, blk=blk: e.tensor_scalar(
                out=rot[:, 0, j * 256:(j + 1) * 256], in0=cf[:, CF_INVF:CF_INVF + 256],
                scalar1=posf[:, blk:blk + 1], scalar2=None, op0=ALU.mult),
                reads=[const_r, res("posf")], writes=[rr])
        for which, shift, dst in (("sin", 0.0, 3), ("cos", float(np.pi / 2), 2)):
            P.op("dve", lambda e, shift=shift: e.tensor_scalar(
                out=t1f, in0=rot[:, 0, :], scalar1=shift, scalar2=float(1.0 / (2 * np.pi)),
                op0=ALU.add, op1=ALU.mult), reads=[rr], writes=t1rs)
            P.op("dve", lambda e: e.tensor_copy(out=roti[:], in_=t1f), reads=t1rs, writes=[res("roti")])
            P.op("dve", lambda e: e.tensor_copy(out=t2f, in_=roti[:]), reads=[res("roti")], writes=t2rs)
            P.op("dve", lambda e, shift=shift: e.tensor_scalar(
                out=t1f, in0=rot[:, 0, :], scalar1=shift, scalar2=None, op0=ALU.add),
                reads=[rr], writes=t1rs)
            P.op("dve", lambda e: e.scalar_tensor_tensor(
                out=rot[:, 1, :], in0=t2f, scalar=-TWO_PI_HI, in1=t1f,
                op0=ALU.mult, op1=ALU.add), reads=t1rs + t2rs, writes=[rr])
            P.op("dve", lambda e: e.scalar_tensor_tensor(
                out=rot[:, 1, :], in0=t2f, scalar=-TWO_PI_LO, in1=rot[:, 1, :],
                op0=ALU.mult, op1=ALU.add), reads=t2rs + [rr], writes=[rr])
            P.op("dve", lambda e: e.tensor_scalar(
                out=rot[:, 1, :], in0=rot[:, 1, :], scalar1=3.1415925, scalar2=-3.1415925,
                op0=ALU.min, op1=ALU.max), reads=[rr], writes=[rr])
            P.op("act", lambda e, dst=dst: e.activation(out=rot[:, dst, :], in_=rot[:, 1, :], func=AF.Sin),
                 reads=[rr], writes=[res("rtab")])

    def mixer_proj(ti):
        u = xn
        def f_chain():
            b = next_bank()
            wap, wr = ws.take(1, [W("w_in", k * 128, 128, 3584, 8, 0, k * 8) for k in range(KC)])
            for k in range(KC):
                mm(banks[b][0:8, 0:T], wap[:, k * 8:(k + 1) * 8], u[:, k, :], k == 0, k == KC - 1, [wr, xn_r[k]], [bank_r[b]])
            fr = res("fsb")
            P.op("act", lambda e, b=b: e.activation(out=fsb[:, 0, :], in_=banks[b][0:8, 0:T], func=AF.Exp,
                                                    bias=cols[0:8, C_NEGB:C_NEGB + 1], scale=-1.0),
                 reads=[bank_r[b], const_r], writes=[fr])
            P.op("act", lambda e: e.activation(out=fsb[:, 1, :], in_=fsb[:, 0, :], func=AF.Ln, bias=cols[0:8, 62:63], scale=1.0),
                 reads=[fr, const_r], writes=[fr])
            cs = ti % 2
            P.op("dve", lambda e, cs=cs: e.tensor_tensor_scan(out=fsb[:, 2, :], data0=ones8[:], data1=fsb[:, 1, :],
                                                             initial=carry[:, cs:cs + 1], op0=ALU.mult, op1=ALU.add),
                 reads=[fr, res("ones8"), res("carry")], writes=[fr])
            P.op("dve", lambda e, cs=cs: e.tensor_copy(out=carry[:, 1 - cs:2 - cs], in_=fsb[:, 2, T - 1:T]),
                 reads=[fr], writes=[res("carry")])
            c3r = res("c3s")
            P.op("dve", lambda e: e.tensor_scalar(out=c3t[:, 0, :], in0=fsb[:, 2, :], scalar1=-1.0, scalar2=None, op0=ALU.mult),
                 reads=[fr], writes=[res("c3t")])
            P.op("dve", lambda e: e.scalar_tensor_tensor(out=fsb[:, 3, :], in0=fsb[:, 2, :], scalar=-1.0, in1=c3t[:, 0, :],
                                                         op0=ALU.mult, op1=ALU.subtract),
                 reads=[fr, res("c3t")], writes=[fr])
            P.op("dve", lambda e: e.tensor_copy(out=c3t[:, 1, :], in_=fsb[:, 3, :]), reads=[fr], writes=[res("c3t")])
            P.op("dve", lambda e: e.tensor_tensor(out=fsb[:, 0, :], in0=fsb[:, 3, :], in1=c3t[:, 1, :], op=ALU.subtract),
                 reads=[fr, res("c3t")], writes=[fr])
            P.op("dve", lambda e: e.tensor_copy(out=c3t[:, 2, :], in_=fsb[:, 0, :]), reads=[fr], writes=[res("c3t")])
            for r3 in range(3):
                P.op("dve", lambda e, r3=r3: e.tensor_copy(out=c3s[32 * r3:32 * r3 + 8, :], in_=c3t[:, r3, :]),
                     reads=[res("c3t")], writes=[c3r])
            for j in range(NB):
                blk = ti * NB + j
                bt = next_bank()
                P.op("pe", lambda e, bt=bt, j=j: e.transpose(out=banks[bt][:, 0:8], in_=fsb[:, 2, j * 128:(j + 1) * 128],
                                                             identity=cf[0:8, CF_IDENT:CF_IDENT + 8]),
                     reads=[fr, const_r], writes=[bank_r[bt]])
                P.op("dve", lambda e, bt=bt, blk=blk: e.tensor_copy(out=ncol[:, blk, :], in_=banks[bt][:, 0:8]),
                     reads=[bank_r[bt]], writes=[res("ncol")])

        for grp in range(4):
            if grp == 2:
                f_chain()
            bj = [next_bank() for _ in range(NB)]
            for k in range(KC):
                wap, wr = ws.take(4, [W("w_in", k * 128, 128, grp * 512, 512)])
                for j in range(NB):
                    mm(banks[bj[j]][:, :], u[:, k, j * 128:(j + 1) * 128], wap, k == 0, k == KC - 1,
                       [wr, xn_r[k]], [bank_r[bj[j]]])
            for j in range(NB):
                b = bj[j]
                if grp < 2:
                    dst = qr if grp == 0 else kr
                    dr = res(("qr", "kr")[grp] + str(j))
                    x1 = banks[b][:, 0:512:2]
                    x2 = banks[b][:, 1:512:2]
                    cosj = rot[:, 2, j * 256:(j + 1) * 256]
                    sinj = rot[:, 3, j * 256:(j + 1) * 256]
                    tr = [res("t1a"), res("t2a")]
                    P.op("dve", lambda e, x1=x1, cosj=cosj: e.tensor_tensor(out=t1[:, 0, :], in0=x1, in1=cosj, op=ALU.mult),
                         reads=[bank_r[b], res("rtab")], writes=[tr[0]])
                    P.op("dve", lambda e, x2=x2, sinj=sinj: e.tensor_tensor(out=t2[:, 0, :], in0=x2, in1=sinj, op=ALU.mult),
                         reads=[bank_r[b], res("rtab")], writes=[tr[1]])
                    P.op("pool", lambda e, dst=dst, j=j: e.tensor_tensor(out=dst[:, j, 0:512:2], in0=t1[:, 0, :], in1=t2[:, 0, :],
                                                                      op=ALU.subtract),
                         reads=tr, writes=[dr])
                    tr2 = [res("t1b"), res("t2b")]
                    P.op("dve", lambda e, x1=x1, sinj=sinj: e.tensor_tensor(out=t1[:, 1, :], in0=x1, in1=sinj, op=ALU.mult),
                         reads=[bank_r[b], res("rtab")], writes=[tr2[0]])
                    P.op("dve", lambda e, x2=x2, cosj=cosj: e.tensor_tensor(out=t2[:, 1, :], in0=x2, in1=cosj, op=ALU.mult),
                         reads=[bank_r[b], res("rtab")], writes=[tr2[1]])
                    P.op("pool", lambda e, dst=dst, j=j: e.tensor_tensor(out=dst[:, j, 1:512:2], in0=t1[:, 1, :], in1=t2[:, 1, :],
                                                                      op=ALU.add),
                         reads=tr2, writes=[dr])
                    if grp == 1:
                        P.op("pool", lambda e, j=j: e.tensor_tensor(out=kdec[:, j, :], in0=kr[:, j, :],
                                                                    in1=cf[:, CF_KDEC:CF_KDEC + 512], op=ALU.mult),
                             reads=[dr, const_r], writes=[res(f"kdec{j}")])
                elif grp == 2:
                    P.op("act", lambda e, b=b, j=j: e.activation(out=vt[:, j, :], in_=banks[b][:, :], func=AF.Copy),
                         reads=[bank_r[b]], writes=[res(f"vt{j}")])
                else:
                    P.op("act", lambda e, b=b, j=j: e.activation(out=sgate[:, j, :], in_=banks[b][:, :], func=AF.Silu),
                         reads=[bank_r[b]], writes=[res(f"sgate{j}")])
        for mc in range(8):
            b = next_bank()
            for k in range(KC):
                wap, wr = ws.take(1, [W("w_in", k * 128, 128, 2048 + mc * 128, 128)])
                mm(banks[b][:, 0:T], wap, u[:, k, :], k == 0, k == KC - 1, [wr, xn_r[k]], [bank_r[b]])
            if mc < 4:
                pr = mc
                P.op("act", lambda e, b=b, pr=pr: e.mul(out=qpad[0:64, 2 * pr, :], in_=banks[b][0:64, 0:T], mul=0.125),
                     reads=[bank_r[b]], writes=[res(f"qpad{2 * pr}")])
                P.op("act", lambda e, b=b, pr=pr: e.mul(out=qpad[64:128, 2 * pr + 1, :], in_=banks[b][64:128, 0:T], mul=0.125),
                     reads=[bank_r[b]], writes=[res(f"qpad{2 * pr + 1}")])
            else:
                pr = mc - 4
                P.op("act", lambda e, b=b, pr=pr: e.activation(out=kT[:, pr, ti * T:(ti + 1) * T], in_=banks[b][:, 0:T],
                                                               func=AF.Copy),
                     reads=[bank_r[b]], writes=[res("kT")])
        bj = [next_bank() for _ in range(NB)]
        for k in range(KC):
            wap, wr = ws.take(4, [W("w_in", k * 128, 128, 3584 - 512, 512)])
            for j in range(NB):
                mm(banks[bj[j]][:, :], u[:, k, j * 128:(j + 1) * 128], wap, k == 0, k == KC - 1,
                   [wr, xn_r[k]], [bank_r[bj[j]]])
        for j in range(NB):
            blk = ti * NB + j
            P.op("act", lambda e, b=bj[j], blk=blk: e.activation(
                out=vaug[:, blk, :, 0:64], in_=banks[b][:, :].rearrange("p (h e) -> p h e", h=FH), func=AF.Copy),
                reads=[bank_r[bj[j]]], writes=[res("vaug")])
    def retention(ti):
        import os
        kcut = int(os.environ.get("KCUT", "99"))
        for j in range(NB):
            for which, src, sr in ((0, qr, res(f"qr{j}")), (1, kr, res(f"kr{j}"))):
                yield
                hb = which
                for hh in range(RH):
                    P.op("pe", lambda e, hb=hb, hh=hh, src=src, j=j: e.transpose(
                        out=bkb[:, hb * 512 + hh * 128: hb * 512 + (hh + 1) * 128],
                        in_=src[:, j, hh * 128:(hh + 1) * 128], identity=cb[:, CB_IDENT:CB_IDENT + 128]),
                        reads=[sr, res("cb")], writes=[bkb_r[hb]])
                if which == 0:
                    P.op("act", lambda e, j=j: e.activation(out=qT[:, j, :], in_=bkb[:, 0:512], func=AF.Copy),
                         reads=[bkb_r[0]], writes=[res(f"qT{j}")])
                    P.op("dve", lambda e, j=j: e.tensor_tensor(out=qdT[:, j, :], in0=bkb[:, 0:512],
                                                               in1=cf[:, CF_QDEC:CF_QDEC + 512], op=ALU.mult),
                         reads=[bkb_r[0], const_r], writes=[res(f"qdT{j}")])
                else:
                    P.op("act", lambda e, j=j: e.activation(out=kTr[:, j, :], in_=bkb[:, 512:1024], func=AF.Copy),
                         reads=[bkb_r[1]], writes=[res(f"kTr{j}")])
            yield
            bs = 3
            for hh in range(RH):
                sl = slice(hh * 128, (hh + 1) * 128)
                mm(banks[bs][:, sl], kTr[:, j, sl], qT[:, j, sl], True, True,
                   [res(f"kTr{j}"), res(f"qT{j}")], [bank_r[bs]])
            P.op("dve", lambda e, bs=bs: e.tensor_tensor(out=AT[:], in0=banks[bs][:, :], in1=cf[:, CF_MASKT:CF_MASKT + 512],
                                                       op=ALU.mult),
                 reads=[bank_r[bs], const_r], writes=[res("AT")])
            yield
            by = 6
            for hh in range(RH):
                sl = slice(hh * 128, (hh + 1) * 128)
                mm(banks[by][:, sl], AT[:, sl], vt[:, j, sl], True, False, [res("AT"), res(f"vt{j}")], [bank_r[by]])
                mm(banks[by][:, sl], qdT[:, j, sl], Sbf[:, sl], False, True, [res(f"qdT{j}"), res("Sbf")], [bank_r[by]])
            bk = 3
            for hh in range(RH):
                sl = slice(hh * 128, (hh + 1) * 128)
                mm(banks[bk][:, sl], kdec[:, j, sl], vt[:, j, sl], True, True, [res(f"kdec{j}"), res(f"vt{j}")], [bank_r[bk]])
            for hh in range(RH):
                sl = slice(hh * 128, (hh + 1) * 128)
                gam = 1.0 - 2.0 ** (-5.0 - hh)
                P.op("dve", lambda e, bk=bk, sl=sl, gam=gam: e.scalar_tensor_tensor(
                    out=Sst[:, sl], in0=Sst[:, sl], scalar=float(gam ** 128), in1=banks[bk][:, sl],
                    op0=ALU.mult, op1=ALU.add), reads=[bank_r[bk], res("Sst")], writes=[res("Sst")])
            P.op("dve", lambda e: e.tensor_copy(out=Sbf[:], in_=Sst[:]), reads=[res("Sst")], writes=[res("Sbf")])
            P.op("act", lambda e, by=by: e.activation(out=ycp[:], in_=banks[by][:, :], func=AF.Copy),
                 reads=[bank_r[by]], writes=[res("ycp")])
            P.op("pool", lambda e: e.tensor_tensor(out=yn[:], in0=ycp[:], in1=ycp[:], op=ALU.mult),
                 reads=[res("ycp")], writes=[res("yn")])
            s4 = res("st4")
            P.op("dve", lambda e: e.tensor_reduce(out=st4[:, 0:4], in_=ycp[:].rearrange("p (h e) -> p h e", h=RH),
                                                  axis=AX.X, op=ALU.add), reads=[res("ycp")], writes=[s4])
            P.op("dve", lambda e: e.tensor_reduce(out=st4[:, 4:8], in_=yn[:].rearrange("p (h e) -> p h e", h=RH),
                                                  axis=AX.X, op=ALU.add), reads=[res("yn")], writes=[s4])
            P.op("dve", lambda e: e.tensor_scalar(out=st4[:, 8:12], in0=st4[:, 0:4], scalar1=1.0 / 128, scalar2=None,
                                                  op0=ALU.mult), reads=[s4], writes=[s4])
            P.op("dve", lambda e: e.tensor_tensor(out=st4[:, 12:16], in0=st4[:, 8:12], in1=st4[:, 8:12], op=ALU.mult),
                 reads=[s4], writes=[s4])
            P.op("dve", lambda e: e.scalar_tensor_tensor(out=st4[:, 16:20], in0=st4[:, 4:8], scalar=1.0 / 128,
                                                         in1=st4[:, 12:16], op0=ALU.mult, op1=ALU.subtract),
                 reads=[s4], writes=[s4])
            P.op("act", lambda e: e.activation(out=st4[:, 20:24], in_=st4[:, 16:20], func=AF.Sqrt, bias=cols[:, 63:64],
                                               scale=1.0), reads=[s4, const_r], writes=[s4])
            P.op("dve", lambda e: e.reciprocal(out=st4[:, 24:28], in_=st4[:, 20:24]), reads=[s4], writes=[s4])
            P.op("dve", lambda e: e.scalar_tensor_tensor(out=st4[:, 28:32], in0=st4[:, 8:12], scalar=-1.0,
                                                         in1=st4[:, 24:28], op0=ALU.mult, op1=ALU.mult),
                 reads=[s4], writes=[s4])
            for hh in range(RH):
                sl = slice(hh * 128, (hh + 1) * 128)
                P.op("dve", lambda e, sl=sl, hh=hh: e.tensor_scalar(
                    out=yn[:, sl], in0=ycp[:, sl], scalar1=st4[:, 24 + hh:25 + hh], scalar2=st4[:, 28 + hh:29 + hh],
                    op0=ALU.mult, op1=ALU.add), reads=[s4, res("ycp")], writes=[res("yn")])
            P.op("dve", lambda e, j=j: e.tensor_tensor(out=yg[:], in0=yn[:], in1=sgate[:, j, :], op=ALU.mult),
                 reads=[res("yn"), res(f"sgate{j}")], writes=[res("yg")])
            yield
            for hh in range(RH):
                P.op("pe", lambda e, hh=hh: e.transpose(out=bkb[:, hh * 128:(hh + 1) * 128],
                                                        in_=yg[:, hh * 128:(hh + 1) * 128],
                                                        identity=cb[:, CB_IDENT:CB_IDENT + 128]),
                     reads=[res("yg"), res("cb")], writes=[bkb_r[0]])
            P.op("act", lambda e, j=j: e.activation(
                out=yretT[:, :, j * 128:(j + 1) * 128], in_=bkb[:, 0:512].rearrange("p (h c) -> p h c", h=RH),
                func=AF.Copy), reads=[bkb_r[0]], writes=[res("yretT")])

    def fox(ti, ret_gen=None):
        nblk = (ti + 1) * NB
        items = [(hh, j) for hh in range(FH) for j in range(nblk)]
        LOOK = 2

        sb_ctr = [0]
        sbank = {}

        def qk(idx):
            hh, j = items[idx]
            jj = j - ti * NB
            n0 = 0 if jj < 0 else jj * 128
            b = idx % 3
            pr = hh // 2
            mm(banks[b][:, n0:T], kT[:, pr, j * 128:(j + 1) * 128], qpad[:, hh, n0:T], True, False,
               [res("kT"), res(f"qpad{hh}")], [bank_r[b]])
            mm(banks[b][:, n0:T], cb[0:96, CB_SEL + hh * 128:CB_SEL + (hh + 1) * 128], c3s[:, n0:T], False, jj < 0,
               [res("cb"), res("c3s")], [bank_r[b]])
            if jj >= 0:
                mm(banks[b][:, n0:n0 + 128], cb[:, CB_IDENT:CB_IDENT + 128], cb[:, CB_TRI:CB_TRI + 128], False, True,
                   [res("cb")], [bank_r[b]])
            s = idx % 4
            P.op("act", lambda e, b=b, s=s, n0=n0, j=j, hh=hh: e.activation(
                out=pt[:, s, n0:T], in_=banks[b][:, n0:T], func=AF.Exp, bias=ncol[:, j, hh:hh + 1], scale=1.0),
                reads=[bank_r[b], res("ncol")], writes=[res(f"pt{s}")])

        pending = []

        def pv(idx):
            hh, j = items[idx]
            jj = j - ti * NB
            n0 = 0 if jj < 0 else jj * 128
            s = idx % 4
            ob = 4 + (hh % 2)
            if j == 0:
                for ent in [p_ for p_ in pending if p_[2] == hh % 2]:
                    pending.remove(ent)
                    ent[1]()
            mm(banks[ob][0:65, n0:T], vaug[:, j, hh, 0:65], pt[:, s, n0:T], j == 0, j == nblk - 1,
               [res("vaug"), res(f"pt{s}")], [bank_r[ob]])
            if j == nblk - 1:
                finish_a(hh, ob)
                pending.append((idx + 6, lambda hh=hh, ob=ob: finish_b(hh, ob), hh % 2))

        def finish_a(hh, ob):
            s = hh % 2
            r2 = res(f"rl2{s}")
            P.op("act", lambda e: e.activation(out=osb[:, s, :], in_=banks[ob][0:64, 0:T], func=AF.Copy),
                 reads=[bank_r[ob]], writes=[res(f"osb{s}")])
            P.op("dve", lambda e: e.reciprocal(out=rl[0:1, s, :], in_=banks[ob][64:65, 0:T]),
                 reads=[bank_r[ob]], writes=[res(f"rl{s}")])
            P.op("dve", lambda e: e.tensor_copy(out=rl2[0:1, s, :], in_=rl[0:1, s, :]),
                 reads=[res(f"rl{s}")], writes=[r2])
            P.op("dve", lambda e: e.tensor_tensor(out=fsb[0:1, 0, :], in0=rl[0:1, s, :], in1=rl2[0:1, s, :], op=ALU.subtract),
                 reads=[res(f"rl{s}"), r2], writes=[res("fsb")])
            P.op("dve", lambda e: e.tensor_copy(out=rl2[32:33, s, :], in_=fsb[0:1, 0, :]),
                 reads=[res("fsb")], writes=[r2])

        def finish_b(hh, ob):
            s = hh % 2
            mm(banks[ob][:, T:2 * T], cb[:, CB_RSEL:CB_RSEL + 128], rl2[:, s, :], True, True,
               [res("cb"), res(f"rl2{s}")], [bank_r[ob]])
            pr = hh // 2
            lo = 64 * (hh % 2)
            P.op("dve", lambda e: e.tensor_tensor(out=yfoxT[lo:lo + 64, pr, :], in0=banks[ob][0:64, T:2 * T], in1=osb[:, s, :],
                                                  op=ALU.mult),
                 reads=[bank_r[ob], res(f"osb{s}")], writes=[res("yfoxT")])

        n = len(items)
        stride = max(1, n // 12)

        def tick():
            if ret_gen is not None:
                next(ret_gen, None)

        for i in range(min(LOOK, n)):
            qk(i)
        for i in range(n):
            if i % stride == 0:
                tick()
            if i + LOOK < n:
                qk(i + LOOK)
            while pending and pending[0][0] <= i:
                pending.pop(0)[1]()
            pv(i)
        if ret_gen is not None:
            for _ in ret_gen:
                pass
        while pending:
            pending.pop(0)[1]()

    def merge():
        u = xn
        for m in range(KC):
            ba, bb, bza, bzb = next_bank(), next_bank(), next_bank(), next_bank()
            for k in range(KC):
                wap, wr = ws.take(1, [W("w_merge", k * 128, 128, m * 128, 128)])
                mm(banks[ba][:, 0:T], wap, u[:, k, :], k == 0, k == KC - 1, [wr, xn_r[k]], [bank_r[ba]])
            for k in range(KC):
                wap, wr = ws.take(1, [W("w_merge", k * 128, 128, D + m * 128, 128)])
                mm(banks[bb][:, 0:T], wap, u[:, k, :], k == 0, k == KC - 1, [wr, xn_r[k]], [bank_r[bb]])
            for k in range(4):
                wap, wr = ws.take(1, [W("w_ret_out", k * 128, 128, m * 128, 128)])
                mm(banks[bza][:, 0:T], wap, yretT[:, k, :], k == 0, k == 3, [wr, res("yretT")], [bank_r[bza]])
            for k in range(4):
                wap, wr = ws.take(1, [W("w_fox_out", k * 128, 128, m * 128, 128)])
                mm(banks[bzb][:, 0:T], wap, yfoxT[:, k, :], k == 0, k == 3, [wr, res("yfoxT")], [bank_r[bzb]])
            P.op("act", lambda e, ba=ba, m=m: e.activation(out=sgt[:, 0, :], in_=banks[ba][:, 0:T], func=AF.Sigmoid,
                                                          bias=col(C_BM + m), scale=1.0),
                 reads=[bank_r[ba], const_r], writes=[res("sgt0")])
            P.op("act", lambda e, bb=bb, m=m: e.activation(out=sgt[:, 1, :], in_=banks[bb][:, 0:T], func=AF.Sigmoid,
                                                          bias=col(C_BM + 8 + m), scale=1.0),
                 reads=[bank_r[bb], const_r], writes=[res("sgt1")])
            P.op("dve", lambda e, bza=bza: e.tensor_tensor(out=t1[:, 0, :], in0=banks[bza][:, 0:T], in1=sgt[:, 0, :], op=ALU.mult),
                 reads=[bank_r[bza], res("sgt0")], writes=[res("t1a")])
            P.op("dve", lambda e, bzb=bzb: e.tensor_tensor(out=t2[:, 0, :], in0=banks[bzb][:, 0:T], in1=sgt[:, 1, :], op=ALU.mult),
                 reads=[bank_r[bzb], res("sgt1")], writes=[res("t2a")])
            P.op("dve", lambda e, m=m: e.tensor_tensor(out=mg[:, m, :], in0=t1[:, 0, :], in1=t2[:, 0, :], op=ALU.add),
                 reads=[res("t1a"), res("t2a")], writes=[res(f"mg{m}")])
        for m in range(KC):
            b = next_bank()
            for k in range(KC):
                wap, wr = ws.take(1, [W("w_out", k * 128, 128, m * 128, 128)])
                mm(banks[b][:, 0:T], wap, mg[:, k, :], k == 0, k == KC - 1, [wr, res(f"mg{k}")], [bank_r[b]])
            P.op("dve", lambda e, b=b, m=m: e.tensor_tensor(out=h[:, m, :], in0=banks[b][:, 0:T], in1=h[:, m, :], op=ALU.add),
                 reads=[bank_r[b], h_r[m]], writes=[h_r[m]])
            if m > 0:
                ss_accum(m - 1)
        ss_accum(KC - 1)

    def ple(ti):
        for j in range(NB):
            slot = 0
            emit_p_load(ti, j)
            b = next_bank()
            for c in range(2):
                P.op("pe", lambda e, b=b, c=c, slot=slot: e.transpose(
                    out=banks[b][:, c * 128:(c + 1) * 128], in_=pst[:, slot, c * 128:(c + 1) * 128],
                    identity=cf[:, CF_IDENT:CF_IDENT + 128]),
                    reads=[res(f"pst{slot}"), const_r], writes=[bank_r[b]])
            P.op("act", lambda e, b=b, j=j: e.activation(
                out=pT[:, :, j * 128:(j + 1) * 128], in_=banks[b][:, 0:256].rearrange("p (c t) -> p c t", c=2),
                func=AF.Copy), reads=[bank_r[b]], writes=[res("pT")])
        for m in range(KC):
            bg, be = next_bank(), next_bank()
            for k in range(KC):
                wap, wr = ws.take(1, [W("w_ple_gate", k * 128, 128, m * 128, 128)])
                mm(banks[bg][:, 0:T], wap, xn[:, k, :], k == 0, k == KC - 1, [wr, xn_r[k]], [bank_r[bg]])
            for k in range(2):
                wap, wr = ws.take(1, [W("w_ple", k * 128, 128, m * 128, 128)])
                mm(banks[be][:, 0:T], wap, pT[:, k, :], k == 0, k == 1, [wr, res("pT")], [bank_r[be]])
            s = m % 3
            P.op("act", lambda e, bg=bg, s=s: e.activation(out=sgt[:, s, :], in_=banks[bg][:, 0:T], func=AF.Sigmoid),
                 reads=[bank_r[bg]], writes=[res(f"sgt{s}")])
            P.op("dve", lambda e, be=be, s=s: e.tensor_tensor(out=t1[:, 0, :], in0=banks[be][:, 0:T], in1=sgt[:, s, :], op=ALU.mult),
                 reads=[bank_r[be], res(f"sgt{s}")], writes=[res("t1a")])
            P.op("dve", lambda e, m=m: e.tensor_tensor(out=h[:, m, :], in0=h[:, m, :], in1=t1[:, 0, :], op=ALU.add),
                 reads=[res("t1a"), h_r[m]], writes=[h_r[m]])

    store_ops = []

    def final(ti):
        for j in range(NB):
            slot = 0
            hr = res(f"ho{slot}")
            for half in range(2):
                b = next_bank()
                for cc in range(4):
                    c = half * 4 + cc
                    P.op("pe", lambda e, b=b, cc=cc, c=c, j=j: e.transpose(
                        out=banks[b][:, cc * 128:(cc + 1) * 128], in_=h[:, c, j * 128:(j + 1) * 128],
                        identity=cf[:, CF_IDENT:CF_IDENT + 128]),
                        reads=[h_r[c], const_r], writes=[bank_r[b]])
                P.op("act", lambda e, b=b, half=half, slot=slot: e.activation(
                    out=ho[:, slot, half * 512:(half + 1) * 512], in_=banks[b][:, :], func=AF.Copy),
                    reads=[bank_r[b]], writes=[hr])
            fr = res("fst")
            P.op("act", lambda e, slot=slot: e.activation(out=rot[:, 0:2, :],
                                                         in_=ho[:, slot, :].rearrange("p (a b) -> p a b", a=2),
                                                         func=AF.Square, accum_out=fst[:, 0:1]),
                 reads=[hr], writes=[res("rotw"), fr])
            P.op("act", lambda e: e.activation(out=fst[:, 1:2], in_=fst[:, 0:1], func=AF.Sqrt, bias=cols[:, 63:64],
                                               scale=1.0 / D), reads=[fr, const_r], writes=[fr])
            P.op("dve", lambda e: e.reciprocal(out=fst[:, 2:3], in_=fst[:, 1:2]), reads=[fr], writes=[fr])
            P.op("dve", lambda e, slot=slot: e.scalar_tensor_tensor(
                out=ho[:, slot, :], in0=ho[:, slot, :], scalar=fst[:, 2:3], in1=lnf[:], op0=ALU.mult, op1=ALU.mult),
                reads=[hr, fr, const_r], writes=[hr])
            t0 = ti * T + j * 128
            o = P.op("pool", lambda e, slot=slot, t0=t0: e.dma_start(out=out_d[t0:t0 + 128, :], in_=ho[:, slot, :]),
                     reads=[hr], dma=hr)
            store_ops.append(o)

    init()
    emit_x_load(0, 0)
    emit_x_load(0, 1)
    emit_p_load(0, 0)
    import os
    kstop = int(os.environ.get("KSTOP", "99"))
    PHASE_LOG.clear()
    for ti in range(NT):
        P.epoch = ti // 2
        phases = [
            lambda: x_transpose(ti),
            lambda: rmsnorm(C_G1),
            lambda: ffn("w_ffn1_gate", "w_ffn1_up", "w_ffn1_down"),
            lambda: rmsnorm(C_GMIX),
            lambda: rotary_tables(ti),
            lambda: mixer_proj(ti),
            lambda: None,
            lambda: fox(ti, retention(ti)),
            lambda: merge(),
            lambda: rmsnorm(C_G2),
            lambda: ffn("w_ffn2_gate", "w_ffn2_up", "w_ffn2_down"),
            lambda: rmsnorm(C_GPLE),
            lambda: ple(ti),
        ]
        for pi, ph in enumerate(phases):
            if pi >= kstop:
                break
            ph()
            PHASE_LOG.append((ti, pi, len(P.ops["pe"])))
            if pi == 5 and ti + 1 < NT:
                emit_x_load(ti + 1, 0)
                emit_x_load(ti + 1, 1)
        if ti + 1 < NT:
            emit_p_load(ti + 1, 0)
        final(ti)
        if kstop >= 99:
            ws.end_tile()
        else:
            ws.recording = False
            ws.blk = 0
            ws.tile += 1
    P.op("pool", None, extra=store_ops)

    for eng in P.ops:
        for o in P.ops[eng]:
            for d in o.deps:
                if not d.is_dma:
                    d.signal = True
    n_epochs = (NT + 1) // 2
    for eng in COMPUTE:
        cnt = {}
        for o in P.ops[eng]:
            if o.signal and not o.is_dma:
                cnt[o.epoch] = cnt.get(o.epoch, 0) + 1
                o.count = cnt[o.epoch]
    prog_sem = {}
    for eng in COMPUTE:
        for ep in range(n_epochs):
            prog_sem[(eng, ep)] = st.enter_context(nc.semaphore(f"p_{eng}_{ep}"))
    dma_res = []
    for eng in P.ops:
        for o in P.ops[eng]:
            if o.is_dma and eng not in o.dres.dsem:
                o.dres.dsem[eng] = st.enter_context(nc.semaphore(f"d_{o.dres.name}_{eng}"))
                dma_res.append(o.dres)

    def token(d):
        if d.is_dma:
            return d.dres.dsem[d.eng], d.dval
        return prog_sem[(d.eng, d.epoch)], d.count

    def emit(e, eng):
        waited = {}
        for o in P.ops[eng]:
            for d in o.deps:
                sem, val = token(d)
                key = id(sem)
                if waited.get(key, 0) >= val:
                    continue
                waited[key] = val
                e.wait_ge(sem, val)
            if o.fn is None:
                continue
            inst = o.fn(e)
            if o.is_dma:
                inst.then_inc(o.dres.dsem[eng], 16)
            elif o.signal:
                inst.then_inc(prog_sem[(eng, o.epoch)], 1)

    with nc.Block() as block:
        @block.tensor
        def _(e):
            emit(e, "pe")

        @block.scalar
        def _(e):
            emit(e, "act")

        @block.vector
        def _(e):
            emit(e, "dve")

        @block.gpsimd
        def _(e):
            emit(e, "pool")

        @block.sync
        def _(e):
            emit(e, "sp")

    st.close()
    stats = {e: len(P.ops[e]) for e in P.ops}
    return nc, plan, NU, stats


def count_units():
    n = 0

    def take(k, cnt=1):
        nonlocal n
        for _ in range(cnt):
            if n % k:
                n += k - n % k
            n += k

    def ffn_():
        take(1, FC * 2 * KC)
        take(1, KC * FC)

    ffn_()
    take(4, 2 * KC)
    take(1, 1)
    take(4, 2 * KC)
    take(1, 8 * KC)
    take(4, KC)
    take(1, KC * (KC + KC + 4 + 4))
    take(1, KC * KC)
    ffn_()
    take(1, KC * (KC + 2))
    return (n + UNIT_BLOCKS - 1) // UNIT_BLOCKS


def make_constants():
    cf = np.zeros((128, 2048), np.float32)
    cf[:, 0:128] = np.eye(128, dtype=np.float32)
    idx = np.arange(128, dtype=np.float64)
    for hh in range(RH):
        lg = np.log1p(-np.exp2(-5.0 - hh))
        diff = idx[None, :] - idx[:, None]
        maskT = np.where(diff >= 0, np.exp(lg * np.maximum(diff, 0.0)), 0.0) * (RD ** -0.5)
        cf[:, 128 + hh * 128:128 + (hh + 1) * 128] = maskT.astype(np.float32)
        cf[:, 640 + hh * 128:640 + (hh + 1) * 128] = np.exp(lg * (idx + 1.0)).astype(np.float32)[None, :]
        cf[:, 1152 + hh * 128:1152 + (hh + 1) * 128] = (np.exp(lg * (127.0 - idx)) * (RD ** -0.5)).astype(np.float32)[:, None]
    half = RD // 2
    inv_freq = (1.0 / (10000.0 ** (np.arange(half, dtype=np.float32) / half))).astype(np.float32)
    cf[:, 1664:1920] = np.tile(inv_freq, 4)[None, :]
    cf[:, 1920:1984] = 1.0
    cb = np.zeros((128, 1536), np.float32)
    cb[:, 0:128] = np.eye(128, dtype=np.float32)
    cb[:, 128:256] = 1.0
    s_i = np.arange(128)[:, None]
    t_i = np.arange(128)[None, :]
    cb[:, 256:384] = np.where(s_i <= t_i, 0.0, MASK_NEG).astype(np.float32)
    for hh in range(FH):
        for r in range(3):
            cb[32 * r + hh, 384 + hh * 128:384 + (hh + 1) * 128] = 1.0
    cb[0, 1408:1536] = 1.0
    cb[32, 1408:1536] = 1.0
    return cf, cb


def pack_weights(plan, NU, wd):
    wst = np.zeros((NU, 128, UNIT_COLS), np.float32)
    for (blk, n, pieces) in plan:
        u = blk // UNIT_BLOCKS
        off = (blk % UNIT_BLOCKS) * 128
        for (name, r0, nr, c0, ncw, dr, dc) in pieces:
            wst[u, dr:dr + nr, off + dc:off + dc + ncw] = wd[name][r0:r0 + nr, c0:c0 + ncw]
    return wst


_CACHE = {}
PHASE_LOG = []


def _get_program(S):
    if S not in _CACHE:
        _CACHE[S] = build_program(S)
    return _CACHE[S]


def kernel(x, p, positions, ln_ffn1, w_ffn1_gate, w_ffn1_up, w_ffn1_down, ln_mix, w_in, b_forget, w_merge,
           b_merge, w_ret_out, w_fox_out, w_out, ln_ffn2, w_ffn2_gate, w_ffn2_up, w_ffn2_down, ln_ple, w_ple,
           w_ple_gate, ln_final, n_cores=None):
    x = np.asarray(x)
    B, S, _ = x.shape
    if n_cores is None:
        n_cores = B
    nc, plan, NU, stats = _get_program(S)
    f = lambda v: np.asarray(v, dtype=np.float32)
    wd = {
        "w_ffn1_gate": f(w_ffn1_gate)[0], "w_ffn1_up": f(w_ffn1_up)[0], "w_ffn1_down": f(w_ffn1_down)[0],
        "w_in": f(w_in)[0], "w_merge": f(w_merge)[0], "w_ret_out": f(w_ret_out)[0], "w_fox_out": f(w_fox_out)[0],
        "w_out": f(w_out)[0], "w_ffn2_gate": f(w_ffn2_gate)[0], "w_ffn2_up": f(w_ffn2_up)[0],
        "w_ffn2_down": f(w_ffn2_down)[0], "w_ple_gate": f(w_ple_gate)[0], "w_ple": f(w_ple)[0],
    }
    wst = pack_weights(plan, NU, wd)
    cols = np.zeros((128, 64), np.float32)
    for c0, g in ((0, ln_ffn1), (8, ln_mix), (16, ln_ffn2), (24, ln_ple)):
        cols[:, c0:c0 + 8] = f(g)[0].reshape(8, 128).T
    cols[:, 32:48] = f(b_merge)[0].reshape(16, 128).T
    cols[0:8, 48] = -f(b_forget)[0]
    cols[:, 62] = 1.0
    cols[:, 63] = EPS
    lnf = np.ascontiguousarray(np.broadcast_to(f(ln_final)[None, :], (128, D)))
    cf, cb = make_constants()
    p = np.asarray(p)
    positions = np.asarray(positions)
    in_maps = []
    for b in range(n_cores):
        pos_t = np.ascontiguousarray(positions[b].astype(np.int32).reshape(S // 128, 128).T)
        in_maps.append({
            "x": np.ascontiguousarray(x[b], dtype=np.float32),
            "p": np.ascontiguousarray(p[0, b], dtype=np.float32),
            "pos": pos_t, "cols": cols, "lnf": lnf, "cf32": cf, "cbf": cb, "wst": wst,
        })
    res = run_bass_kernel_spmd(nc, in_maps, core_ids=list(range(n_cores)))
    out = np.stack([np.asarray(r["out"], dtype=np.float32) for r in res.results], axis=0)
    return out
```

```python
import numpy as np
from contextlib import ExitStack
import concourse.bass as bass
import concourse.mybir as mybir
from concourse.bass_utils import run_bass_kernel_spmd

F32 = mybir.dt.float32
BF16 = mybir.dt.bfloat16
I32 = mybir.dt.int32
AF = mybir.ActivationFunctionType
ALU = mybir.AluOpType
AX = mybir.AxisListType

D = 1024
DFF = 2816
KC = D // 128
FC = DFF // 128
PLE = 256
RH, RD = 4, 128
FH, FD = 8, 64
T = 256
NB = T // 128
UNIT_BLOCKS = 16
UNIT_COLS = UNIT_BLOCKS * 128
NSLOT = 4
EPS = 1e-6
IN_COLS = 4 * 512 + 3 * 512 + 8
TWO_PI_HI = 6.28125
TWO_PI_LO = float(2.0 * np.pi - 6.28125)
MASK_NEG = -30000.0


class Res:
    __slots__ = ("name", "last_write", "reads", "dsem", "dcount", "excl")

    def __init__(self, name):
        self.name = name
        self.excl = False
        self.last_write = None
        self.reads = []
        self.dsem = {}
        self.dcount = {}


class Op:
    __slots__ = ("eng", "fn", "deps", "epoch", "signal", "count", "is_dma", "dres", "dval", "idx")


COMPUTE = ("pe", "act", "dve", "pool")


class Prog:
    def __init__(self):
        self.ops = {e: [] for e in ("pe", "act", "dve", "pool", "sp")}
        self.epoch = 0
        self.n_ops = 0

    def op(self, eng, fn, reads=(), writes=(), extra=(), dma=None):
        o = Op()
        o.eng = eng
        o.fn = fn
        o.epoch = self.epoch
        o.signal = False
        o.count = 0
        o.is_dma = dma is not None
        o.dres = dma
        o.dval = 0
        o.idx = self.n_ops
        self.n_ops += 1
        deps = []
        xr = [r for r in reads if r.excl]
        if xr:
            reads = [r for r in reads if not r.excl]
            writes = list(writes) + [r for r in xr if r not in writes]
        for r in reads:
            if r.last_write is not None:
                deps.append(r.last_write)
        for w in writes:
            if w.last_write is not None:
                deps.append(w.last_write)
            deps.extend(w.reads)
        deps.extend(extra)
        seen = set()
        od = []
        for d in deps:
            if d is None or d is o or id(d) in seen:
                continue
            seen.add(id(d))
            if (not d.is_dma) and d.eng == "pe" and eng == "pe" and not o.is_dma:
                continue
            od.append(d)
        o.deps = od
        if o.is_dma:
            dma.dcount[eng] = dma.dcount.get(eng, 0) + 1
            o.dval = dma.dcount[eng] * 16
        for r in reads:
            if not o.is_dma:
                r.reads = [x for x in r.reads if x.is_dma or x.eng != eng]
            r.reads.append(o)
        for w in writes:
            w.last_write = o
            w.reads = []
        self.ops[eng].append(o)
        return o


def build_program(S):
    NT = S // T
    NBLK = S // 128
    nc = bass.Bass("TRN2", target_bir_lowering=False)
    P = Prog()
    plan = []
    st = ExitStack()

    def dram_in(name, shape, dt):
        return nc.dram_tensor(name, list(shape), dt, kind="ExternalInput")

    x_d = dram_in("x", [S, D], F32)
    p_d = dram_in("p", [S, PLE], F32)
    pos_d = dram_in("pos", [128, NBLK], I32)
    cols_d = dram_in("cols", [128, 64], F32)
    lnf_d = dram_in("lnf", [128, D], F32)
    cf_d = dram_in("cf32", [128, 2048], F32)
    cb_d = dram_in("cbf", [128, 1536], F32)
    out_d = nc.dram_tensor("out", [S, D], F32, kind="ExternalOutput")

    def sb(name, shape, dt):
        return st.enter_context(nc.sbuf_tensor("s_" + name, list(shape), dt))

    def ps(name, shape, dt):
        return st.enter_context(nc.psum_tensor(name, list(shape), dt))

    kT = sb("kT", [128, 4, S], BF16)
    vaug = sb("vaug", [128, NBLK, FH, 66], BF16)
    ncol = sb("ncol", [128, NBLK, 8], F32)
    Sst = sb("Sst", [128, 512], F32)
    Sbf = sb("Sbf", [128, 512], BF16)
    cols = sb("cols", [128, 64], F32)
    lnf = sb("lnf", [128, D], F32)
    cf = sb("cf", [128, 2048], F32)
    cb = sb("cb", [128, 1536], BF16)
    posi = sb("posi", [128, NBLK], I32)
    posf = sb("posf", [128, NBLK], F32)
    carry = sb("carry", [8, 2], F32)
    ones8 = sb("ones8", [8, T], F32)
    c3s = sb("c3s", [96, T], BF16)
    c3t = sb("c3t", [8, 3, T], BF16)
    wring = sb("wring", [128, NSLOT, UNIT_COLS], BF16)
    h = sb("h", [128, KC, T], F32)
    xn = sb("xn", [128, KC, T], BF16)
    a = sb("a", [128, FC, T], BF16)
    xs = sb("xs", [128, 2, D], F32)
    sq = sb("sq", [128, 2, T], BF16)
    sd = sb("sd", [128, T], F32)
    rstd = sb("rstd", [128, T], F32)
    sgt = sb("sgt", [128, 3, T], F32)
    t1 = sb("t1", [128, 2, T], F32)
    t2 = sb("t2", [128, 2, T], F32)
    qr = sb("qr", [128, NB, 512], BF16)
    kr = sb("kr", [128, NB, 512], BF16)
    kdec = sb("kdec", [128, NB, 512], BF16)
    vt = sb("vt", [128, NB, 512], BF16)
    sgate = sb("sgate", [128, NB, 512], F32)
    qT = sb("qT", [128, NB, 512], BF16)
    qdT = sb("qdT", [128, NB, 512], BF16)
    kTr = sb("kTr", [128, NB, 512], BF16)
    AT = sb("AT", [128, 512], BF16)
    ycp = sb("ycp", [128, 512], F32)
    yn = sb("yn", [128, 512], F32)
    yg = sb("yg", [128, 512], BF16)
    st4 = sb("st4", [128, 32], F32)
    yretT = sb("yretT", [128, 4, T], BF16)
    yfoxT = sb("yfoxT", [128, 4, T], BF16)
    qpad = sb("qpad", [128, FH, T], BF16)
    fsb = sb("fsb", [8, 4, T], F32)
    pt = sb("pt", [128, 4, T], BF16)
    osb = sb("osb", [64, 2, T], F32)
    rl = sb("rl", [1, 2, T], F32)
    rl2 = sb("rl2", [128, 2, T], BF16)
    mg = sb("mg", [128, KC, T], BF16)
    pst = sb("pst", [128, 1, PLE], F32)
    pT = sb("pT", [128, 2, T], BF16)
    rot = sb("rot", [128, 4, 512], F32)
    roti = sb("roti", [128, 512], I32)
    ho = sb("ho", [128, 1, D], F32)
    fst = sb("fst", [128, 8], F32)
    banks = [ps(f"bk{i}", [128, 512], F32) for i in range(7)]
    bkb = ps("bkb", [128, 1024], BF16)

    R = {}

    def res(name):
        if name not in R:
            R[name] = Res(name)
        return R[name]

    bank_r = [res(f"bank{i}") for i in range(7)]
    bkb_r = [res("bkb0"), res("bkb0")]
    for r_ in bank_r + bkb_r:
        r_.excl = True
    slot_r = [res(f"slot{i}") for i in range(NSLOT)]
    h_r = [res(f"h{c}") for c in range(KC)]
    a_r = [res(f"a{c}") for c in range(FC)]
    const_r = res("const")
    xn_r = [res(f"xn{c}") for c in range(KC)]

    C_G1, C_GMIX, C_G2, C_GPLE = 0, 8, 16, 24
    C_BM = 32
    C_NEGB = 48
    CF_IDENT = 0
    CF_MASKT = 128
    CF_QDEC = 640
    CF_KDEC = 1152
    CF_INVF = 1664
    CF_ONES = 1920
    CB_IDENT = 0
    CB_ONES = 128
    CB_TRI = 256
    CB_SEL = 384
    CB_RSEL = 1408

    class WS:
        def __init__(self):
            self.blk = 0
            self.tile = 0
            self.loaded = 0
            self.nu = None
            self.wb_ops = {}
            self.recording = True

        def units_per_tile(self):
            return self.nu

        def ensure(self, gunit):
            lim = gunit + NSLOT - 1
            total = None if self.nu is None else self.nu * NT
            while self.loaded <= lim and (total is None or self.loaded < total):
                self.emit_load(self.loaded)
                self.loaded += 1

        def emit_load(self, g):
            slot = g % NSLOT
            sr = slot_r[slot]
            if self.nu is None or g < self.nu:
                u = g
                src = wst_d[u]
                P.op("pool", lambda e, s=slot, sr_=src: e.dma_start(out=wring[:, s, :], in_=sr_),
                     writes=[sr], dma=sr)
                o = P.op("sp", lambda e, s=slot, u_=u: e.dma_start(out=wscr_d[u_], in_=wring[:, s, :]),
                         reads=[sr], dma=sr)
                self.wb_ops[u] = o
            else:
                u = g % self.nu
                P.op("sp", lambda e, s=slot, u_=u: e.dma_start(out=wring[:, s, :], in_=wscr_d[u_]),
                     writes=[sr], extra=[self.wb_ops[u]], dma=sr)

        def take(self, n, pieces):
            if self.blk % n:
                self.blk += n - self.blk % n
            if self.recording:
                plan.append((self.blk, n, pieces))
            base = 0 if self.nu is None else self.tile * self.nu
            u = self.blk // UNIT_BLOCKS
            g = base + u
            self.ensure(g)
            off = (self.blk % UNIT_BLOCKS) * 128
            slot = g % NSLOT
            self.blk += n
            return wring[:, slot, off:off + n * 128], slot_r[slot]

        def end_tile(self):
            nu = (self.blk + UNIT_BLOCKS - 1) // UNIT_BLOCKS
            if self.nu is None:
                self.nu = nu
            assert nu == self.nu
            self.recording = False
            self.blk = 0
            self.tile += 1

    ws = WS()
    NU = count_units()
    wst_d = dram_in("wst", [NU, 128, UNIT_COLS], F32)
    wscr_d = nc.dram_tensor("wscr", [NU, 128, UNIT_COLS], BF16, kind="Internal")
    ws.nu = NU

    bank_rot = [0]

    def next_bank(lo=0, hi=6):
        b = bank_rot[0]
        if b < lo or b >= hi:
            b = lo
        bank_rot[0] = b + 1
        return b

    def mm(out, lhsT, rhs, start, stop, reads, writes):
        P.op("pe", lambda e: e.matmul(out, lhsT=lhsT, rhs=rhs, start=start, stop=stop),
             reads=reads, writes=writes)

    def W(name, r0, nr, c0, ncw, dr=0, dc=0):
        return (name, r0, nr, c0, ncw, dr, dc)

    def col(c):
        return cols[:, c:c + 1]

    def init():
        for (dst, src) in ((cols, cols_d), (lnf, lnf_d), (cf, cf_d), (posi, pos_d)):
            P.op("sp", lambda e, d_=dst, s_=src: e.dma_start(out=d_[:], in_=s_.ap()),
                 writes=[const_r], dma=const_r)
        P.op("pool", lambda e: e.dma_start(out=cb[:], in_=cb_d.ap()), writes=[res("cb")], dma=res("cb"))
        P.op("dve", lambda e: e.tensor_copy(out=posf[:], in_=posi[:]), reads=[const_r], writes=[res("posf")])
        P.op("dve", lambda e: e.memset(Sst[:], 0.0), writes=[res("Sst")])
        P.op("dve", lambda e: e.memset(Sbf[:], 0.0), writes=[res("Sbf")])
        P.op("dve", lambda e: e.memset(carry[:], 0.0), writes=[res("carry")])
        P.op("dve", lambda e: e.memset(ones8[:], 1.0), writes=[res("ones8")])
        P.op("dve", lambda e: e.memset(c3s[:], 0.0), writes=[res("c3s")])
        P.op("dve", lambda e: e.memset(rl2[:], 0.0), writes=[res("rl20"), res("rl21")])
        P.op("dve", lambda e: e.memset(qpad[:], 0.0), writes=[res(f"qpad{i}") for i in range(FH)])
        P.op("pool", lambda e: e.memset(vaug[:], 1.0), writes=[res("vaug")])

    done_loads = set()

    def emit_x_load(ti, j):
        if ("x", ti, j) in done_loads:
            return
        done_loads.add(("x", ti, j))
        slot = j % 2
        t0 = ti * T + j * 128
        P.op("pool", lambda e: e.dma_start(out=xs[:, slot, :], in_=x_d[t0:t0 + 128, :]),
             writes=[res(f"xs{slot}")], dma=res(f"xs{slot}"))

    def emit_p_load(ti, j):
        if ("p", ti, j) in done_loads:
            return
        done_loads.add(("p", ti, j))
        slot = 0
        t0 = ti * T + j * 128
        P.op("pool", lambda e: e.dma_start(out=pst[:, slot, :], in_=p_d[t0:t0 + 128, :]),
             writes=[res(f"pst{slot}")], dma=res(f"pst{slot}"))

    def x_transpose(ti):
        for j in range(NB):
            slot = j % 2
            emit_x_load(ti, j)
            for half in range(2):
                b = next_bank()
                for cc in range(4):
                    c = half * 4 + cc
                    P.op("pe", lambda e, b=b, cc=cc, c=c, slot=slot: e.transpose(
                        out=banks[b][:, cc * 128:(cc + 1) * 128], in_=xs[:, slot, c * 128:(c + 1) * 128],
                        identity=cf[:, CF_IDENT:CF_IDENT + 128]),
                        reads=[res(f"xs{slot}"), const_r], writes=[bank_r[b]])
                P.op("act", lambda e, b=b, half=half, j=j: e.activation(
                    out=h[:, half * 4:half * 4 + 4, j * 128:(j + 1) * 128],
                    in_=banks[b][:, :].rearrange("p (c t) -> p c t", c=4), func=AF.Copy),
                    reads=[bank_r[b]], writes=[h_r[half * 4 + i] for i in range(4)])
                if j == NB - 1:
                    for cc in range(4):
                        ss_accum(half * 4 + cc)

    def ss_accum(c):
        s_ = c % 2
        if c % 2 == 0:
            P.op("act", lambda e: e.activation(out=sq[:, s_, :], in_=h[:, c, :], func=AF.Square),
                 reads=[h_r[c]], writes=[res(f"sq{s_}")])
        else:
            P.op("dve", lambda e: e.tensor_tensor(out=sq[:, s_, :], in0=h[:, c, :], in1=h[:, c, :], op=ALU.mult),
                 reads=[h_r[c]], writes=[res(f"sq{s_}")])
        mm(banks[6][:, 0:T], cb[:, CB_ONES:CB_ONES + 128], sq[:, s_, :], c == 0, c == KC - 1,
           [res(f"sq{s_}"), res("cb")], [bank_r[6]])

    def rmsnorm(gcol0):
        b = 6
        P.op("act", lambda e: e.activation(out=sd[:], in_=banks[b][:, 0:T], func=AF.Sqrt,
                                           bias=cols[:, 63:64], scale=1.0 / D),
             reads=[bank_r[b], const_r], writes=[res("sd")])
        P.op("dve", lambda e: e.reciprocal(out=rstd[:], in_=sd[:]), reads=[res("sd")], writes=[res("rstd")])
        for c in range(KC):
            P.op("dve", lambda e, c=c: e.scalar_tensor_tensor(
                out=xn[:, c, :], in0=h[:, c, :], scalar=col(gcol0 + c), in1=rstd[:],
                op0=ALU.mult, op1=ALU.mult),
                reads=[h_r[c], res("rstd"), const_r], writes=[xn_r[c]])

    def ffn(wg, wu, wd):
        for m in range(FC):
            bg = next_bank()
            bu = next_bank()
            for k in range(KC):
                wap, wr = ws.take(1, [W(wg, k * 128, 128, m * 128, 128)])
                mm(banks[bg][:, 0:T], wap, xn[:, k, :], k == 0, k == KC - 1, [wr, xn_r[k]], [bank_r[bg]])
            for k in range(KC):
                wap, wr = ws.take(1, [W(wu, k * 128, 128, m * 128, 128)])
                mm(banks[bu][:, 0:T], wap, xn[:, k, :], k == 0, k == KC - 1, [wr, xn_r[k]], [bank_r[bu]])
            s = m % 3
            P.op("act", lambda e, bg=bg, s=s: e.activation(out=sgt[:, s, :], in_=banks[bg][:, 0:T], func=AF.Silu),
                 reads=[bank_r[bg]], writes=[res(f"sgt{s}")])
            P.op("dve", lambda e, bu=bu, s=s, m=m: e.tensor_tensor(out=a[:, m, :], in0=banks[bu][:, 0:T],
                                                                  in1=sgt[:, s, :], op=ALU.mult),
                 reads=[bank_r[bu], res(f"sgt{s}")], writes=[a_r[m]])
        for m in range(KC):
            b = next_bank()
            for k in range(FC):
                wap, wr = ws.take(1, [W(wd, k * 128, 128, m * 128, 128)])
                mm(banks[b][:, 0:T], wap, a[:, k, :], k == 0, k == FC - 1, [wr, a_r[k]], [bank_r[b]])
            P.op("dve", lambda e, b=b, m=m: e.scalar_tensor_tensor(
                out=h[:, m, :], in0=banks[b][:, 0:T], scalar=0.5, in1=h[:, m, :], op0=ALU.mult, op1=ALU.add),
                reads=[bank_r[b], h_r[m]], writes=[h_r[m]])
            if m > 0:
                ss_accum(m - 1)
        ss_accum(KC - 1)

    def rotary_tables(ti):
        rr = res("rotw")
        t1f = t1[:].rearrange("p a b -> p (a b)")
        t2f = t2[:].rearrange("p a b -> p (a b)")
        t1rs = [res("t1a"), res("t1b")]
        t2rs = [res("t2a"), res("t2b")]
        for j in range(NB):
            blk = ti * NB + j
            P.op("dve", lambda e, j=j, blk=blk: e.tensor_scalar(
                out=rot[:, 0, j * 256:(j + 1) * 256], in0=cf[:, CF_INVF:CF_INVF + 256],
                scalar1=posf[:, blk:blk + 1], scalar2=None, op0=ALU.mult),
                reads=[const_r, res("posf")], writes=[rr])
        for which, shift, dst in (("sin", 0.0, 3), ("cos", float(np.pi / 2), 2)):
            P.op("dve", lambda e, shift=shift: e.tensor_scalar(
                out=t1f, in0=rot[:, 0, :], scalar1=shift, scalar2=float(1.0 / (2 * np.pi)),
                op0=ALU.add, op1=ALU.mult), reads=[rr], writes=t1rs)
            P.op("dve", lambda e: e.tensor_copy(out=roti[:], in_=t1f), reads=t1rs, writes=[res("roti")])
            P.op("dve", lambda e: e.tensor_copy(out=t2f, in_=roti[:]), reads=[res("roti")], writes=t2rs)
            P.op("dve", lambda e, shift=shift: e.tensor_scalar(
                out=t1f, in0=rot[:, 0, :], scalar1=shift, scalar2=None, op0=ALU.add),
                reads=[rr], writes=t1rs)
            P.op("dve", lambda e: e.scalar_tensor_tensor(
                out=rot[:, 1, :], in0=t2f, scalar=-TWO_PI_HI, in1=t1f,
                op0=ALU.mult, op1=ALU.add), reads=t1rs + t2rs, writes=[rr])
            P.op("dve", lambda e: e.scalar_tensor_tensor(
                out=rot[:, 1, :], in0=t2f, scalar=-TWO_PI_LO, in1=rot[:, 1, :],
                op0=ALU.mult, op1=ALU.add), reads=t2rs + [rr], writes=[rr])
            P.op("dve", lambda e: e.tensor_scalar(
                out=rot[:, 1, :], in0=rot[:, 1, :], scalar1=3.1415925, scalar2=-3.1415925,
                op0=ALU.min, op1=ALU.max), reads=[rr], writes=[rr])
            P.op("act", lambda e, dst=dst: e.activation(out=rot[:, dst, :], in_=rot[:, 1, :], func=AF.Sin),
                 reads=[rr], writes=[res("rtab")])

    def mixer_proj(ti):
        u = xn
        def f_chain():
            b = next_bank()
            wap, wr = ws.take(1, [W("w_in", k * 128, 128, 3584, 8, 0, k * 8) for k in range(KC)])
            for k in range(KC):
                mm(banks[b][0:8, 0:T], wap[:, k * 8:(k + 1) * 8], u[:, k, :], k == 0, k == KC - 1, [wr, xn_r[k]], [bank_r[b]])
            fr = res("fsb")
            P.op("act", lambda e, b=b: e.activation(out=fsb[:, 0, :], in_=banks[b][0:8, 0:T], func=AF.Exp,
                                                    bias=cols[0:8, C_NEGB:C_NEGB + 1], scale=-1.0),
                 reads=[bank_r[b], const_r], writes=[fr])
            P.op("act", lambda e: e.activation(out=fsb[:, 1, :], in_=fsb[:, 0, :], func=AF.Ln, bias=cols[0:8, 62:63], scale=1.0),
                 reads=[fr, const_r], writes=[fr])
            cs = ti % 2
            P.op("dve", lambda e, cs=cs: e.tensor_tensor_scan(out=fsb[:, 2, :], data0=ones8[:], data1=fsb[:, 1, :],
                                                             initial=carry[:, cs:cs + 1], op0=ALU.mult, op1=ALU.add),
                 reads=[fr, res("ones8"), res("carry")], writes=[fr])
            P.op("dve", lambda e, cs=cs: e.tensor_copy(out=carry[:, 1 - cs:2 - cs], in_=fsb[:, 2, T - 1:T]),
                 reads=[fr], writes=[res("carry")])
            c3r = res("c3s")
            P.op("dve", lambda e: e.tensor_scalar(out=c3t[:, 0, :], in0=fsb[:, 2, :], scalar1=-1.0, scalar2=None, op0=ALU.mult),
                 reads=[fr], writes=[res("c3t")])
            P.op("dve", lambda e: e.scalar_tensor_tensor(out=fsb[:, 3, :], in0=fsb[:, 2, :], scalar=-1.0, in1=c3t[:, 0, :],
                                                         op0=ALU.mult, op1=ALU.subtract),
                 reads=[fr, res("c3t")], writes=[fr])
            P.op("dve", lambda e: e.tensor_copy(out=c3t[:, 1, :], in_=fsb[:, 3, :]), reads=[fr], writes=[res("c3t")])
            P.op("dve", lambda e: e.tensor_tensor(out=fsb[:, 0, :], in0=fsb[:, 3, :], in1=c3t[:, 1, :], op=ALU.subtract),
                 reads=[fr, res("c3t")], writes=[fr])
            P.op("dve", lambda e: e.tensor_copy(out=c3t[:, 2, :], in_=fsb[:, 0, :]), reads=[fr], writes=[res("c3t")])
            for r3 in range(3):
                P.op("dve", lambda e, r3=r3: e.tensor_copy(out=c3s[32 * r3:32 * r3 + 8, :], in_=c3t[:, r3, :]),
                     reads=[res("c3t")], writes=[c3r])
            for j in range(NB):
                blk = ti * NB + j
                bt = next_bank()
                P.op("pe", lambda e, bt=bt, j=j: e.transpose(out=banks[bt][:, 0:8], in_=fsb[:, 2, j * 128:(j + 1) * 128],
                                                             identity=cf[0:8, CF_IDENT:CF_IDENT + 8]),
                     reads=[fr, const_r], writes=[bank_r[bt]])
                P.op("dve", lambda e, bt=bt, blk=blk: e.tensor_copy(out=ncol[:, blk, :], in_=banks[bt][:, 0:8]),
                     reads=[bank_r[bt]], writes=[res("ncol")])

        for grp in range(4):
            if grp == 2:
                f_chain()
            bj = [next_bank() for _ in range(NB)]
            for k in range(KC):
                wap, wr = ws.take(4, [W("w_in", k * 128, 128, grp * 512, 512)])
                for j in range(NB):
                    mm(banks[bj[j]][:, :], u[:, k, j * 128:(j + 1) * 128], wap, k == 0, k == KC - 1,
                       [wr, xn_r[k]], [bank_r[bj[j]]])
            for j in range(NB):
                b = bj[j]
                if grp < 2:
                    dst = qr if grp == 0 else kr
                    dr = res(("qr", "kr")[grp] + str(j))
                    x1 = banks[b][:, 0:512:2]
                    x2 = banks[b][:, 1:512:2]
                    cosj = rot[:, 2, j * 256:(j + 1) * 256]
                    sinj = rot[:, 3, j * 256:(j + 1) * 256]
                    tr = [res("t1a"), res("t2a")]
                    P.op("dve", lambda e, x1=x1, cosj=cosj: e.tensor_tensor(out=t1[:, 0, :], in0=x1, in1=cosj, op=ALU.mult),
                         reads=[bank_r[b], res("rtab")], writes=[tr[0]])
                    P.op("dve", lambda e, x2=x2, sinj=sinj: e.tensor_tensor(out=t2[:, 0, :], in0=x2, in1=sinj, op=ALU.mult),
                         reads=[bank_r[b], res("rtab")], writes=[tr[1]])
                    P.op("pool", lambda e, dst=dst, j=j: e.tensor_tensor(out=dst[:, j, 0:512:2], in0=t1[:, 0, :], in1=t2[:, 0, :],
                                                                      op=ALU.subtract),
                         reads=tr, writes=[dr])
                    tr2 = [res("t1b"), res("t2b")]
                    P.op("dve", lambda e, x1=x1, sinj=sinj: e.tensor_tensor(out=t1[:, 1, :], in0=x1, in1=sinj, op=ALU.mult),
                         reads=[bank_r[b], res("rtab")], writes=[tr2[0]])
                    P.op("dve", lambda e, x2=x2, cosj=cosj: e.tensor_tensor(out=t2[:, 1, :], in0=x2, in1=cosj, op=ALU.mult),
                         reads=[bank_r[b], res("rtab")], writes=[tr2[1]])
                    P.op("pool", lambda e, dst=dst, j=j: e.tensor_tensor(out=dst[:, j, 1:512:2], in0=t1[:, 1, :], in1=t2[:, 1, :],
                                                                      op=ALU.add),
                         reads=tr2, writes=[dr])
                    if grp == 1:
                        P.op("pool", lambda e, j=j: e.tensor_tensor(out=kdec[:, j, :], in0=kr[:, j, :],
                                                                    in1=cf[:, CF_KDEC:CF_KDEC + 512], op=ALU.mult),
                             reads=[dr, const_r], writes=[res(f"kdec{j}")])
                elif grp == 2:
                    P.op("act", lambda e, b=b, j=j: e.activation(out=vt[:, j, :], in_=banks[b][:, :], func=AF.Copy),
                         reads=[bank_r[b]], writes=[res(f"vt{j}")])
                else:
                    P.op("act", lambda e, b=b, j=j: e.activation(out=sgate[:, j, :], in_=banks[b][:, :], func=AF.Silu),
                         reads=[bank_r[b]], writes=[res(f"sgate{j}")])
        for mc in range(8):
            b = next_bank()
            for k in range(KC):
                wap, wr = ws.take(1, [W("w_in", k * 128, 128, 2048 + mc * 128, 128)])
                mm(banks[b][:, 0:T], wap, u[:, k, :], k == 0, k == KC - 1, [wr, xn_r[k]], [bank_r[b]])
            if mc < 4:
                pr = mc
                P.op("act", lambda e, b=b, pr=pr: e.mul(out=qpad[0:64, 2 * pr, :], in_=banks[b][0:64, 0:T], mul=0.125),
                     reads=[bank_r[b]], writes=[res(f"qpad{2 * pr}")])
                P.op("act", lambda e, b=b, pr=pr: e.mul(out=qpad[64:128, 2 * pr + 1, :], in_=banks[b][64:128, 0:T], mul=0.125),
                     reads=[bank_r[b]], writes=[res(f"qpad{2 * pr + 1}")])
            else:
                pr = mc - 4
                P.op("act", lambda e, b=b, pr=pr: e.activation(out=kT[:, pr, ti * T:(ti + 1) * T], in_=banks[b][:, 0:T],
                                                               func=AF.Copy),
                     reads=[bank_r[b]], writes=[res("kT")])
        bj = [next_bank() for _ in range(NB)]
        for k in range(KC):
            wap, wr = ws.take(4, [W("w_in", k * 128, 128, 3584 - 512, 512)])
            for j in range(NB):
                mm(banks[bj[j]][:, :], u[:, k, j * 128:(j + 1) * 128], wap, k == 0, k == KC - 1,
                   [wr, xn_r[k]], [bank_r[bj[j]]])
        for j in range(NB):
            blk = ti * NB + j
            P.op("act", lambda e, b=bj[j], blk=blk: e.activation(
                out=vaug[:, blk, :, 0:64], in_=banks[b][:, :].rearrange("p (h e) -> p h e", h=FH), func=AF.Copy),
                reads=[bank_r[bj[j]]], writes=[res("vaug")])
    def retention(ti):
        import os
        kcut = int(os.environ.get("KCUT", "99"))
        for j in range(NB):
            for which, src, sr in ((0, qr, res(f"qr{j}")), (1, kr, res(f"kr{j}"))):
                yield
                hb = which
                for hh in range(RH):
                    P.op("pe", lambda e, hb=hb, hh=hh, src=src, j=j: e.transpose(
                        out=bkb[:, hb * 512 + hh * 128: hb * 512 + (hh + 1) * 128],
                        in_=src[:, j, hh * 128:(hh + 1) * 128], identity=cb[:, CB_IDENT:CB_IDENT + 128]),
                        reads=[sr, res("cb")], writes=[bkb_r[hb]])
                if which == 0:
                    P.op("act", lambda e, j=j: e.activation(out=qT[:, j, :], in_=bkb[:, 0:512], func=AF.Copy),
                         reads=[bkb_r[0]], writes=[res(f"qT{j}")])
                    P.op("dve", lambda e, j=j: e.tensor_tensor(out=qdT[:, j, :], in0=bkb[:, 0:512],
                                                               in1=cf[:, CF_QDEC:CF_QDEC + 512], op=ALU.mult),
                         reads=[bkb_r[0], const_r], writes=[res(f"qdT{j}")])
                else:
                    P.op("act", lambda e, j=j: e.activation(out=kTr[:, j, :], in_=bkb[:, 512:1024], func=AF.Copy),
                         reads=[bkb_r[1]], writes=[res(f"kTr{j}")])
            yield
            bs = 3
            for hh in range(RH):
                sl = slice(hh * 128, (hh + 1) * 128)
                mm(banks[bs][:, sl], kTr[:, j, sl], qT[:, j, sl], True, True,
                   [res(f"kTr{j}"), res(f"qT{j}")], [bank_r[bs]])
            P.op("dve", lambda e, bs=bs: e.tensor_tensor(out=AT[:], in0=banks[bs][:, :], in1=cf[:, CF_MASKT:CF_MASKT + 512],
                                                       op=ALU.mult),
                 reads=[bank_r[bs], const_r], writes=[res("AT")])
            yield
            by = 6
            for hh in range(RH):
                sl = slice(hh * 128, (hh + 1) * 128)
                mm(banks[by][:, sl], AT[:, sl], vt[:, j, sl], True, False, [res("AT"), res(f"vt{j}")], [bank_r[by]])
                mm(banks[by][:, sl], qdT[:, j, sl], Sbf[:, sl], False, True, [res(f"qdT{j}"), res("Sbf")], [bank_r[by]])
            bk = 3
            for hh in range(RH):
                sl = slice(hh * 128, (hh + 1) * 128)
                mm(banks[bk][:, sl], kdec[:, j, sl], vt[:, j, sl], True, True, [res(f"kdec{j}"), res(f"vt{j}")], [bank_r[bk]])
            for hh in range(RH):
                sl = slice(hh * 128, (hh + 1) * 128)
                gam = 1.0 - 2.0 ** (-5.0 - hh)
                P.op("dve", lambda e, bk=bk, sl=sl, gam=gam: e.scalar_tensor_tensor(
                    out=Sst[:, sl], in0=Sst[:, sl], scalar=float(gam ** 128), in1=banks[bk][:, sl],
                    op0=ALU.mult, op1=ALU.add), reads=[bank_r[bk], res("Sst")], writes=[res("Sst")])
            P.op("dve", lambda e: e.tensor_copy(out=Sbf[:], in_=Sst[:]), reads=[res("Sst")], writes=[res("Sbf")])
            yield
            P.op("act", lambda e, by=by: e.activation(out=ycp[:], in_=banks[by][:, :], func=AF.Copy),
                 reads=[bank_r[by]], writes=[res("ycp")])
            P.op("dve", lambda e: e.tensor_tensor(out=yn[:], in0=ycp[:], in1=ycp[:], op=ALU.mult),
                 reads=[res("ycp")], writes=[res("yn")])
            s4 = res("st4")
            P.op("dve", lambda e: e.tensor_reduce(out=st4[:, 0:4], in_=ycp[:].rearrange("p (h e) -> p h e", h=RH),
                                                  axis=AX.X, op=ALU.add), reads=[res("ycp")], writes=[s4])
            P.op("dve", lambda e: e.tensor_reduce(out=st4[:, 4:8], in_=yn[:].rearrange("p (h e) -> p h e", h=RH),
                                                  axis=AX.X, op=ALU.add), reads=[res("yn")], writes=[s4])
            P.op("dve", lambda e: e.tensor_scalar(out=st4[:, 8:12], in0=st4[:, 0:4], scalar1=1.0 / 128, scalar2=None,
                                                  op0=ALU.mult), reads=[s4], writes=[s4])
            P.op("dve", lambda e: e.tensor_tensor(out=st4[:, 12:16], in0=st4[:, 8:12], in1=st4[:, 8:12], op=ALU.mult),
                 reads=[s4], writes=[s4])
            P.op("dve", lambda e: e.scalar_tensor_tensor(out=st4[:, 16:20], in0=st4[:, 4:8], scalar=1.0 / 128,
                                                         in1=st4[:, 12:16], op0=ALU.mult, op1=ALU.subtract),
                 reads=[s4], writes=[s4])
            yield
            P.op("act", lambda e: e.activation(out=st4[:, 20:24], in_=st4[:, 16:20], func=AF.Sqrt, bias=cols[:, 63:64],
                                               scale=1.0), reads=[s4, const_r], writes=[s4])
            P.op("dve", lambda e: e.reciprocal(out=st4[:, 24:28], in_=st4[:, 20:24]), reads=[s4], writes=[s4])
            P.op("dve", lambda e: e.scalar_tensor_tensor(out=st4[:, 28:32], in0=st4[:, 8:12], scalar=-1.0,
                                                         in1=st4[:, 24:28], op0=ALU.mult, op1=ALU.mult),
                 reads=[s4], writes=[s4])
            for hh in range(RH):
                sl = slice(hh * 128, (hh + 1) * 128)
                P.op("dve", lambda e, sl=sl, hh=hh: e.tensor_scalar(
                    out=yn[:, sl], in0=ycp[:, sl], scalar1=st4[:, 24 + hh:25 + hh], scalar2=st4[:, 28 + hh:29 + hh],
                    op0=ALU.mult, op1=ALU.add), reads=[s4, res("ycp")], writes=[res("yn")])
            P.op("dve", lambda e, j=j: e.tensor_tensor(out=yg[:], in0=yn[:], in1=sgate[:, j, :], op=ALU.mult),
                 reads=[res("yn"), res(f"sgate{j}")], writes=[res("yg")])
            yield
            for hh in range(RH):
                P.op("pe", lambda e, hh=hh: e.transpose(out=bkb[:, hh * 128:(hh + 1) * 128],
                                                        in_=yg[:, hh * 128:(hh + 1) * 128],
                                                        identity=cb[:, CB_IDENT:CB_IDENT + 128]),
                     reads=[res("yg"), res("cb")], writes=[bkb_r[0]])
            P.op("act", lambda e, j=j: e.activation(
                out=yretT[:, :, j * 128:(j + 1) * 128], in_=bkb[:, 0:512].rearrange("p (h c) -> p h c", h=RH),
                func=AF.Copy), reads=[bkb_r[0]], writes=[res("yretT")])

    def fox(ti, ret_gen=None):
        nblk = (ti + 1) * NB
        items = [(hh, j) for hh in range(FH) for j in range(nblk)]
        LOOK = 2

        sb_ctr = [0]
        sbank = {}

        def qk(idx):
            hh, j = items[idx]
            jj = j - ti * NB
            n0 = 0 if jj < 0 else jj * 128
            b = idx % 3
            pr = hh // 2
            mm(banks[b][:, n0:T], kT[:, pr, j * 128:(j + 1) * 128], qpad[:, hh, n0:T], True, False,
               [res("kT"), res(f"qpad{hh}")], [bank_r[b]])
            mm(banks[b][:, n0:T], cb[0:96, CB_SEL + hh * 128:CB_SEL + (hh + 1) * 128], c3s[:, n0:T], False, jj < 0,
               [res("cb"), res("c3s")], [bank_r[b]])
            if jj >= 0:
                mm(banks[b][:, n0:n0 + 128], cb[:, CB_IDENT:CB_IDENT + 128], cb[:, CB_TRI:CB_TRI + 128], False, True,
                   [res("cb")], [bank_r[b]])
            s = idx % 4
            P.op("act", lambda e, b=b, s=s, n0=n0, j=j, hh=hh: e.activation(
                out=pt[:, s, n0:T], in_=banks[b][:, n0:T], func=AF.Exp, bias=ncol[:, j, hh:hh + 1], scale=1.0),
                reads=[bank_r[b], res("ncol")], writes=[res(f"pt{s}")])

        pending = []

        def pv(idx):
            hh, j = items[idx]
            jj = j - ti * NB
            n0 = 0 if jj < 0 else jj * 128
            s = idx % 4
            ob = 4 + (hh % 2)
            if j == 0:
                for ent in [p_ for p_ in pending if p_[2] == hh % 2]:
                    pending.remove(ent)
                    ent[1]()
            mm(banks[ob][0:65, n0:T], vaug[:, j, hh, 0:65], pt[:, s, n0:T], j == 0, j == nblk - 1,
               [res("vaug"), res(f"pt{s}")], [bank_r[ob]])
            if j == nblk - 1:
                finish_a(hh, ob)
                pending.append((idx + 6, lambda hh=hh, ob=ob: finish_b(hh, ob), hh % 2))

        def finish_a(hh, ob):
            s = hh % 2
            r2 = res(f"rl2{s}")
            P.op("act", lambda e: e.activation(out=osb[:, s, :], in_=banks[ob][0:64, 0:T], func=AF.Copy),
                 reads=[bank_r[ob]], writes=[res(f"osb{s}")])
            P.op("dve", lambda e: e.reciprocal(out=rl[0:1, s, :], in_=banks[ob][64:65, 0:T]),
                 reads=[bank_r[ob]], writes=[res(f"rl{s}")])
            P.op("dve", lambda e: e.tensor_copy(out=rl2[0:1, s, :], in_=rl[0:1, s, :]),
                 reads=[res(f"rl{s}")], writes=[r2])
            P.op("dve", lambda e: e.tensor_tensor(out=fsb[0:1, 0, :], in0=rl[0:1, s, :], in1=rl2[0:1, s, :], op=ALU.subtract),
                 reads=[res(f"rl{s}"), r2], writes=[res("fsb")])
            P.op("dve", lambda e: e.tensor_copy(out=rl2[32:33, s, :], in_=fsb[0:1, 0, :]),
                 reads=[res("fsb")], writes=[r2])

        def finish_b(hh, ob):
            s = hh % 2
            mm(banks[ob][:, T:2 * T], cb[:, CB_RSEL:CB_RSEL + 128], rl2[:, s, :], True, True,
               [res("cb"), res(f"rl2{s}")], [bank_r[ob]])
            pr = hh // 2
            lo = 64 * (hh % 2)
            P.op("dve", lambda e: e.tensor_tensor(out=yfoxT[lo:lo + 64, pr, :], in0=banks[ob][0:64, T:2 * T], in1=osb[:, s, :],
                                                  op=ALU.mult),
                 reads=[bank_r[ob], res(f"osb{s}")], writes=[res("yfoxT")])

        n = len(items)
        stride = max(1, n // 17)

        def tick():
            if ret_gen is not None:
                next(ret_gen, None)

        for i in range(min(LOOK, n)):
            qk(i)
        for i in range(n):
            if i % stride == 0:
                tick()
            if i + LOOK < n:
                qk(i + LOOK)
            while pending and pending[0][0] <= i:
                pending.pop(0)[1]()
            pv(i)
        if ret_gen is not None:
            for _ in ret_gen:
                pass
        while pending:
            pending.pop(0)[1]()

    def merge():
        u = xn
        for m in range(KC):
            ba, bb, bza, bzb = next_bank(), next_bank(), next_bank(), next_bank()
            for k in range(KC):
                wap, wr = ws.take(1, [W("w_merge", k * 128, 128, m * 128, 128)])
                mm(banks[ba][:, 0:T], wap, u[:, k, :], k == 0, k == KC - 1, [wr, xn_r[k]], [bank_r[ba]])
            for k in range(KC):
                wap, wr = ws.take(1, [W("w_merge", k * 128, 128, D + m * 128, 128)])
                mm(banks[bb][:, 0:T], wap, u[:, k, :], k == 0, k == KC - 1, [wr, xn_r[k]], [bank_r[bb]])
            for k in range(4):
                wap, wr = ws.take(1, [W("w_ret_out", k * 128, 128, m * 128, 128)])
                mm(banks[bza][:, 0:T], wap, yretT[:, k, :], k == 0, k == 3, [wr, res("yretT")], [bank_r[bza]])
            for k in range(4):
                wap, wr = ws.take(1, [W("w_fox_out", k * 128, 128, m * 128, 128)])
                mm(banks[bzb][:, 0:T], wap, yfoxT[:, k, :], k == 0, k == 3, [wr, res("yfoxT")], [bank_r[bzb]])
            P.op("act", lambda e, ba=ba, m=m: e.activation(out=sgt[:, 0, :], in_=banks[ba][:, 0:T], func=AF.Sigmoid,
                                                          bias=col(C_BM + m), scale=1.0),
                 reads=[bank_r[ba], const_r], writes=[res("sgt0")])
            P.op("act", lambda e, bb=bb, m=m: e.activation(out=sgt[:, 1, :], in_=banks[bb][:, 0:T], func=AF.Sigmoid,
                                                          bias=col(C_BM + 8 + m), scale=1.0),
                 reads=[bank_r[bb], const_r], writes=[res("sgt1")])
            P.op("dve", lambda e, bza=bza: e.tensor_tensor(out=t1[:, 0, :], in0=banks[bza][:, 0:T], in1=sgt[:, 0, :], op=ALU.mult),
                 reads=[bank_r[bza], res("sgt0")], writes=[res("t1a")])
            P.op("dve", lambda e, bzb=bzb: e.tensor_tensor(out=t2[:, 0, :], in0=banks[bzb][:, 0:T], in1=sgt[:, 1, :], op=ALU.mult),
                 reads=[bank_r[bzb], res("sgt1")], writes=[res("t2a")])
            P.op("dve", lambda e, m=m: e.tensor_tensor(out=mg[:, m, :], in0=t1[:, 0, :], in1=t2[:, 0, :], op=ALU.add),
                 reads=[res("t1a"), res("t2a")], writes=[res(f"mg{m}")])
        for m in range(KC):
            b = next_bank()
            for k in range(KC):
                wap, wr = ws.take(1, [W("w_out", k * 128, 128, m * 128, 128)])
                mm(banks[b][:, 0:T], wap, mg[:, k, :], k == 0, k == KC - 1, [wr, res(f"mg{k}")], [bank_r[b]])
            P.op("dve", lambda e, b=b, m=m: e.tensor_tensor(out=h[:, m, :], in0=banks[b][:, 0:T], in1=h[:, m, :], op=ALU.add),
                 reads=[bank_r[b], h_r[m]], writes=[h_r[m]])
            if m > 0:
                ss_accum(m - 1)
        ss_accum(KC - 1)

    def ple(ti):
        for j in range(NB):
            slot = 0
            emit_p_load(ti, j)
            b = next_bank()
            for c in range(2):
                P.op("pe", lambda e, b=b, c=c, slot=slot: e.transpose(
                    out=banks[b][:, c * 128:(c + 1) * 128], in_=pst[:, slot, c * 128:(c + 1) * 128],
                    identity=cf[:, CF_IDENT:CF_IDENT + 128]),
                    reads=[res(f"pst{slot}"), const_r], writes=[bank_r[b]])
            P.op("act", lambda e, b=b, j=j: e.activation(
                out=pT[:, :, j * 128:(j + 1) * 128], in_=banks[b][:, 0:256].rearrange("p (c t) -> p c t", c=2),
                func=AF.Copy), reads=[bank_r[b]], writes=[res("pT")])
        for m in range(KC):
            bg, be = next_bank(), next_bank()
            for k in range(KC):
                wap, wr = ws.take(1, [W("w_ple_gate", k * 128, 128, m * 128, 128)])
                mm(banks[bg][:, 0:T], wap, xn[:, k, :], k == 0, k == KC - 1, [wr, xn_r[k]], [bank_r[bg]])
            for k in range(2):
                wap, wr = ws.take(1, [W("w_ple", k * 128, 128, m * 128, 128)])
                mm(banks[be][:, 0:T], wap, pT[:, k, :], k == 0, k == 1, [wr, res("pT")], [bank_r[be]])
            s = m % 3
            P.op("act", lambda e, bg=bg, s=s: e.activation(out=sgt[:, s, :], in_=banks[bg][:, 0:T], func=AF.Sigmoid),
                 reads=[bank_r[bg]], writes=[res(f"sgt{s}")])
            P.op("dve", lambda e, be=be, s=s: e.tensor_tensor(out=t1[:, 0, :], in0=banks[be][:, 0:T], in1=sgt[:, s, :], op=ALU.mult),
                 reads=[bank_r[be], res(f"sgt{s}")], writes=[res("t1a")])
            P.op("dve", lambda e, m=m: e.tensor_tensor(out=h[:, m, :], in0=h[:, m, :], in1=t1[:, 0, :], op=ALU.add),
                 reads=[res("t1a"), h_r[m]], writes=[h_r[m]])

    store_ops = []

    def final(ti):
        for j in range(NB):
            slot = 0
            hr = res(f"ho{slot}")
            for half in range(2):
                b = next_bank()
                for cc in range(4):
                    c = half * 4 + cc
                    P.op("pe", lambda e, b=b, cc=cc, c=c, j=j: e.transpose(
                        out=banks[b][:, cc * 128:(cc + 1) * 128], in_=h[:, c, j * 128:(j + 1) * 128],
                        identity=cf[:, CF_IDENT:CF_IDENT + 128]),
                        reads=[h_r[c], const_r], writes=[bank_r[b]])
                P.op("act", lambda e, b=b, half=half, slot=slot: e.activation(
                    out=ho[:, slot, half * 512:(half + 1) * 512], in_=banks[b][:, :], func=AF.Copy),
                    reads=[bank_r[b]], writes=[hr])
            fr = res("fst")
            P.op("act", lambda e, slot=slot: e.activation(out=rot[:, 0:2, :],
                                                         in_=ho[:, slot, :].rearrange("p (a b) -> p a b", a=2),
                                                         func=AF.Square, accum_out=fst[:, 0:1]),
                 reads=[hr], writes=[res("rotw"), fr])
            P.op("act", lambda e: e.activation(out=fst[:, 1:2], in_=fst[:, 0:1], func=AF.Sqrt, bias=cols[:, 63:64],
                                               scale=1.0 / D), reads=[fr, const_r], writes=[fr])
            P.op("dve", lambda e: e.reciprocal(out=fst[:, 2:3], in_=fst[:, 1:2]), reads=[fr], writes=[fr])
            P.op("dve", lambda e, slot=slot: e.scalar_tensor_tensor(
                out=ho[:, slot, :], in0=ho[:, slot, :], scalar=fst[:, 2:3], in1=lnf[:], op0=ALU.mult, op1=ALU.mult),
                reads=[hr, fr, const_r], writes=[hr])
            t0 = ti * T + j * 128
            o = P.op("pool", lambda e, slot=slot, t0=t0: e.dma_start(out=out_d[t0:t0 + 128, :], in_=ho[:, slot, :]),
                     reads=[hr], dma=hr)
            store_ops.append(o)

    init()
    emit_x_load(0, 0)
    emit_x_load(0, 1)
    emit_p_load(0, 0)
    import os
    kstop = int(os.environ.get("KSTOP", "99"))
    PHASE_LOG.clear()
    for ti in range(NT):
        P.epoch = ti // 2
        phases = [
            lambda: x_transpose(ti),
            lambda: rmsnorm(C_G1),
            lambda: ffn("w_ffn1_gate", "w_ffn1_up", "w_ffn1_down"),
            lambda: rmsnorm(C_GMIX),
            lambda: rotary_tables(ti),
            lambda: mixer_proj(ti),
            lambda: None,
            lambda: fox(ti, retention(ti)),
            lambda: merge(),
            lambda: rmsnorm(C_G2),
            lambda: ffn("w_ffn2_gate", "w_ffn2_up", "w_ffn2_down"),
            lambda: rmsnorm(C_GPLE),
            lambda: ple(ti),
        ]
        for pi, ph in enumerate(phases):
            if pi >= kstop:
                break
            ph()
            PHASE_LOG.append((ti, pi, len(P.ops["pe"])))
            if pi == 5 and ti + 1 < NT:
                emit_x_load(ti + 1, 0)
                emit_x_load(ti + 1, 1)
        if ti + 1 < NT:
            emit_p_load(ti + 1, 0)
        final(ti)
        if kstop >= 99:
            ws.end_tile()
        else:
            ws.recording = False
            ws.blk = 0
            ws.tile += 1
    P.op("pool", None, extra=store_ops)

    for eng in P.ops:
        for o in P.ops[eng]:
            for d in o.deps:
                if not d.is_dma:
                    d.signal = True
    n_epochs = (NT + 1) // 2
    for eng in COMPUTE:
        cnt = {}
        for o in P.ops[eng]:
            if o.signal and not o.is_dma:
                cnt[o.epoch] = cnt.get(o.epoch, 0) + 1
                o.count = cnt[o.epoch]
    prog_sem = {}
    for eng in COMPUTE:
        for ep in range(n_epochs):
            prog_sem[(eng, ep)] = st.enter_context(nc.semaphore(f"p_{eng}_{ep}"))
    dma_res = []
    for eng in P.ops:
        for o in P.ops[eng]:
            if o.is_dma and eng not in o.dres.dsem:
                o.dres.dsem[eng] = st.enter_context(nc.semaphore(f"d_{o.dres.name}_{eng}"))
                dma_res.append(o.dres)

    def token(d):
        if d.is_dma:
            return d.dres.dsem[d.eng], d.dval
        return prog_sem[(d.eng, d.epoch)], d.count

    def emit(e, eng):
        waited = {}
        for o in P.ops[eng]:
            for d in o.deps:
                sem, val = token(d)
                key = id(sem)
                if waited.get(key, 0) >= val:
                    continue
                waited[key] = val
                e.wait_ge(sem, val)
            if o.fn is None:
                continue
            inst = o.fn(e)
            if o.is_dma:
                inst.then_inc(o.dres.dsem[eng], 16)
            elif o.signal:
                inst.then_inc(prog_sem[(eng, o.epoch)], 1)

    with nc.Block() as block:
        @block.tensor
        def _(e):
            emit(e, "pe")

        @block.scalar
        def _(e):
            emit(e, "act")

        @block.vector
        def _(e):
            emit(e, "dve")

        @block.gpsimd
        def _(e):
            emit(e, "pool")

        @block.sync
        def _(e):
            emit(e, "sp")

    st.close()
    stats = {e: len(P.ops[e]) for e in P.ops}
    return nc, plan, NU, stats


def count_units():
    n = 0

    def take(k, cnt=1):
        nonlocal n
        for _ in range(cnt):
            if n % k:
                n += k - n % k
            n += k

    def ffn_():
        take(1, FC * 2 * KC)
        take(1, KC * FC)

    ffn_()
    take(4, 2 * KC)
    take(1, 1)
    take(4, 2 * KC)
    take(1, 8 * KC)
    take(4, KC)
    take(1, KC * (KC + KC + 4 + 4))
    take(1, KC * KC)
    ffn_()
    take(1, KC * (KC + 2))
    return (n + UNIT_BLOCKS - 1) // UNIT_BLOCKS


def make_constants():
    cf = np.zeros((128, 2048), np.float32)
    cf[:, 0:128] = np.eye(128, dtype=np.float32)
    idx = np.arange(128, dtype=np.float64)
    for hh in range(RH):
        lg = np.log1p(-np.exp2(-5.0 - hh))
        diff = idx[None, :] - idx[:, None]
        maskT = np.where(diff >= 0, np.exp(lg * np.maximum(diff, 0.0)), 0.0) * (RD ** -0.5)
        cf[:, 128 + hh * 128:128 + (hh + 1) * 128] = maskT.astype(np.float32)
        cf[:, 640 + hh * 128:640 + (hh + 1) * 128] = np.exp(lg * (idx + 1.0)).astype(np.float32)[None, :]
        cf[:, 1152 + hh * 128:1152 + (hh + 1) * 128] = (np.exp(lg * (127.0 - idx)) * (RD ** -0.5)).astype(np.float32)[:, None]
    half = RD // 2
    inv_freq = (1.0 / (10000.0 ** (np.arange(half, dtype=np.float32) / half))).astype(np.float32)
    cf[:, 1664:1920] = np.tile(inv_freq, 4)[None, :]
    cf[:, 1920:1984] = 1.0
    cb = np.zeros((128, 1536), np.float32)
    cb[:, 0:128] = np.eye(128, dtype=np.float32)
    cb[:, 128:256] = 1.0
    s_i = np.arange(128)[:, None]
    t_i = np.arange(128)[None, :]
    cb[:, 256:384] = np.where(s_i <= t_i, 0.0, MASK_NEG).astype(np.float32)
    for hh in range(FH):
        for r in range(3):
            cb[32 * r + hh, 384 + hh * 128:384 + (hh + 1) * 128] = 1.0
    cb[0, 1408:1536] = 1.0
    cb[32, 1408:1536] = 1.0
    return cf, cb


def pack_weights(plan, NU, wd):
    wst = np.zeros((NU, 128, UNIT_COLS), np.float32)
    for (blk, n, pieces) in plan:
        u = blk // UNIT_BLOCKS
        off = (blk % UNIT_BLOCKS) * 128
        for (name, r0, nr, c0, ncw, dr, dc) in pieces:
            wst[u, dr:dr + nr, off + dc:off + dc + ncw] = wd[name][r0:r0 + nr, c0:c0 + ncw]
    return wst


_CACHE = {}
PHASE_LOG = []


def _get_program(S):
    if S not in _CACHE:
        _CACHE[S] = build_program(S)
    return _CACHE[S]


def kernel(x, p, positions, ln_ffn1, w_ffn1_gate, w_ffn1_up, w_ffn1_down, ln_mix, w_in, b_forget, w_merge,
           b_merge, w_ret_out, w_fox_out, w_out, ln_ffn2, w_ffn2_gate, w_ffn2_up, w_ffn2_down, ln_ple, w_ple,
           w_ple_gate, ln_final, n_cores=None):
    x = np.asarray(x)
    B, S, _ = x.shape
    if n_cores is None:
        n_cores = B
    nc, plan, NU, stats = _get_program(S)
    f = lambda v: np.asarray(v, dtype=np.float32)
    wd = {
        "w_ffn1_gate": f(w_ffn1_gate)[0], "w_ffn1_up": f(w_ffn1_up)[0], "w_ffn1_down": f(w_ffn1_down)[0],
        "w_in": f(w_in)[0], "w_merge": f(w_merge)[0], "w_ret_out": f(w_ret_out)[0], "w_fox_out": f(w_fox_out)[0],
        "w_out": f(w_out)[0], "w_ffn2_gate": f(w_ffn2_gate)[0], "w_ffn2_up": f(w_ffn2_up)[0],
        "w_ffn2_down": f(w_ffn2_down)[0], "w_ple_gate": f(w_ple_gate)[0], "w_ple": f(w_ple)[0],
    }
    wst = pack_weights(plan, NU, wd)
    cols = np.zeros((128, 64), np.float32)
    for c0, g in ((0, ln_ffn1), (8, ln_mix), (16, ln_ffn2), (24, ln_ple)):
        cols[:, c0:c0 + 8] = f(g)[0].reshape(8, 128).T
    cols[:, 32:48] = f(b_merge)[0].reshape(16, 128).T
    cols[0:8, 48] = -f(b_forget)[0]
    cols[:, 62] = 1.0
    cols[:, 63] = EPS
    lnf = np.ascontiguousarray(np.broadcast_to(f(ln_final)[None, :], (128, D)))
    cf, cb = make_constants()
    p = np.asarray(p)
    positions = np.asarray(positions)
    in_maps = []
    for b in range(n_cores):
        pos_t = np.ascontiguousarray(positions[b].astype(np.int32).reshape(S // 128, 128).T)
        in_maps.append({
            "x": np.ascontiguousarray(x[b], dtype=np.float32),
            "p": np.ascontiguousarray(p[0, b], dtype=np.float32),
            "pos": pos_t, "cols": cols, "lnf": lnf, "cf32": cf, "cbf": cb, "wst": wst,
        })
    res = run_bass_kernel_spmd(nc, in_maps, core_ids=list(range(n_cores)))
    out = np.stack([np.asarray(r["out"], dtype=np.float32) for r in res.results], axis=0)
    return out
```
